# Optimizing a Trainium2 kernel written in Bass

```python
import math
import jax, jax.numpy as jnp
from jax import lax
import numpy as np

D_MODEL = 2048
BATCH = 1
SEQ = 8192
DEPTH = 4

CHUNK = 64
N_EVEN = (DEPTH + 1) // 2
N_ODD = DEPTH // 2
DN_ALPHA = (2 * DEPTH) ** 0.25
DN_BETA = (8 * DEPTH) ** -0.25
LN_EPS = 1e-5

A_WIDTH = D_MODEL // 2
A_HEAD = 64
A_HEADS = A_WIDTH // A_HEAD
A_DECAY_LORA = 64
A_ICL_LORA = 64
A_GATE_LORA = 160
A_SIZES = (A_WIDTH, A_WIDTH, A_WIDTH, A_DECAY_LORA, A_ICL_LORA, A_GATE_LORA)
A_COLS = sum(A_SIZES)
A_GN_EPS = 64e-5

B_WIDTH = D_MODEL // 2
B_BLOCKS = 16
B_BLOCK = B_WIDTH // B_BLOCKS
B_CONV = 4
B_C = 8.0
B_COLS = 2 * B_WIDTH
EVEN_COLS = A_COLS + B_COLS

C_HEADS = 16
C_HEAD_DIM = 128
C_Q_RANK = 512
C_KV_RANK = 256
IDX_HEADS = 16
IDX_DIM = 64
TOPK_MAX = 256
Q_BLOCK = 128
ODD_SIZES = (C_Q_RANK, C_KV_RANK, IDX_DIM, IDX_HEADS)
ODD_COLS = sum(ODD_SIZES)
REL_BUCKETS = 32
REL_MAX_DIST = 128

D_FF = 5632
FFN_CONV = 3
PLE_DIM = 256

kernel_name = "hybrid_rwkv7_rglru_dsa_deepnorm_trunk"


def split_cols(z, sizes):
    return jnp.split(z, np.cumsum(sizes)[:-1].tolist(), axis=-1)


def layer_norm(x, g, b, eps=LN_EPS):
    xf = x.astype(jnp.float32)
    mu = jnp.mean(xf, -1, keepdims=True)
    var = jnp.mean(jnp.square(xf - mu), -1, keepdims=True)
    return ((xf - mu) * lax.rsqrt(var + eps) * g + b).astype(x.dtype)


def rms_norm(x, g, eps=1e-6):
    xf = x.astype(jnp.float32)
    return (xf * lax.rsqrt(jnp.mean(xf * xf, -1, keepdims=True) + eps) * g).astype(x.dtype)


def causal_dwconv(x, w, b):
    width, ch = w.shape
    y = lax.conv_general_dilated(x, w[:, None, :].astype(x.dtype), window_strides=(1,), padding=[(width - 1, 0)], dimension_numbers=('NWC', 'WIO', 'NWC'), feature_group_count=ch)
    return y + b


def token_shift(z):
    return jnp.pad(z, ((0, 0), (1, 0), (0, 0)))[:, :-1]


def t5_bucket(rel):
    nb = REL_BUCKETS // 2
    max_exact = nb // 2
    ret = jnp.where(rel > 0, nb, 0)
    n = jnp.abs(rel)
    nf = jnp.maximum(n, 1).astype(jnp.float32)
    large = max_exact + (jnp.log(nf / max_exact) / math.log(REL_MAX_DIST / max_exact) * (nb - max_exact)).astype(jnp.int32)
    large = jnp.minimum(large, nb - 1)
    return ret + jnp.where(n < max_exact, n, large)


def rwkv7_mix(z, mu, w0, w2, a0, a2, g2, k_k, k_a, r_k, gn_g, gn_b):
    bsz, seq, _ = z.shape
    f32 = jnp.float32
    z = z + (token_shift(z) - z) * mu
    r, k, v, w_lo, a_lo, g_lo = split_cols(z, A_SIZES)
    w = -jax.nn.softplus(-(w0 + jnp.tanh(w_lo) @ w2)) - 0.5
    decay = jnp.exp(-jnp.exp(w.astype(f32)))
    a = jax.nn.sigmoid(a0 + a_lo @ a2)
    g = jax.nn.sigmoid(g_lo) @ g2
    heads = lambda t: t.astype(f32).reshape(bsz, seq, A_HEADS, A_HEAD)
    kk = heads(k * k_k)
    kk = kk / jnp.maximum(jnp.sqrt(jnp.sum(kk * kk, -1, keepdims=True)), 1e-12)
    k = k * (1.0 + (a - 1.0) * k_a)
    r, k, v, a, decay = heads(r), heads(k), heads(v), heads(a), heads(decay)

    def step(state, inp):
        r_t, w_t, k_t, v_t, kk_t, a_t = inp
        sa = jnp.einsum('bhvk,bhk->bhv', state, -kk_t)
        state = state * w_t[:, :, None, :] + sa[..., None] * (kk_t * a_t)[:, :, None, :] + v_t[..., None] * k_t[:, :, None, :]
        return state, jnp.einsum('bhvk,bhk->bhv', state, r_t)

    xs = tuple(jnp.moveaxis(t, 1, 0) for t in (r, decay, k, v, kk, a))
    state0 = jnp.zeros((bsz, A_HEADS, A_HEAD, A_HEAD), f32)
    _, y = lax.scan(step, state0, xs)
    y = jnp.moveaxis(y, 0, 1)
    ym = jnp.mean(y, -1, keepdims=True)
    yv = jnp.mean(jnp.square(y - ym), -1, keepdims=True)
    y = ((y - ym) * lax.rsqrt(yv + A_GN_EPS)).reshape(bsz, seq, A_WIDTH) * gn_g + gn_b
    bonus = (jnp.sum(r * k * r_k, -1, keepdims=True) * v).reshape(bsz, seq, A_WIDTH)
    return ((y + bonus) * g).astype(z.dtype)


def rglru_mix(z, conv_w, conv_b, w_r, b_r, w_i, b_i, lam):
    bsz, seq, _ = z.shape
    f32 = jnp.float32
    xb, gate = jnp.split(z, 2, axis=-1)
    xc = causal_dwconv(xb, conv_w, conv_b)
    xh = xc.reshape(bsz, seq, B_BLOCKS, B_BLOCK)
    r = jax.nn.sigmoid(jnp.einsum('bsnd,nde->bsne', xh, w_r).reshape(bsz, seq, B_WIDTH) + b_r)
    i = jax.nn.sigmoid(jnp.einsum('bsnd,nde->bsne', xh, w_i).reshape(bsz, seq, B_WIDTH) + b_i)
    log_a = -B_C * r.astype(f32) * jax.nn.softplus(-lam.astype(f32))
    a = jnp.exp(log_a)
    u = jnp.sqrt(-jnp.expm1(2.0 * log_a)) * (i * xc).astype(f32)

    def combine(left, right):
        a_l, h_l = left
        a_r, h_r = right
        return a_l * a_r, a_r * h_l + h_r

    _, h = lax.associative_scan(combine, (a, u), axis=1)
    return (jax.nn.gelu(gate) * h.astype(z.dtype)).astype(z.dtype)


def dsa_mix(x, w_in, q_norm, kv_norm, w_uq, w_uk, w_uv, w_qidx, kidx_g, kidx_b, rel_bias):
    bsz, seq, _ = x.shape
    f32 = jnp.float32
    c_q, c_kv, k_idx, w_idx = split_cols(x @ w_in, ODD_SIZES)
    c_q = rms_norm(c_q, q_norm)
    c_kv = rms_norm(c_kv, kv_norm)
    q = (c_q @ w_uq).reshape(bsz, seq, C_HEADS, C_HEAD_DIM)
    q_abs = jnp.einsum('bshd,hrd->bshr', q, w_uk)
    q_idx = (c_q @ w_qidx).reshape(bsz, seq, IDX_HEADS, IDX_DIM)
    k_idx = layer_norm(k_idx, kidx_g, kidx_b)
    w_idx = w_idx * (IDX_HEADS ** -0.5)
    k_sel = min(TOPK_MAX, seq // 4)
    n_blk = seq // Q_BLOCK
    key_chunk = jnp.arange(seq) // CHUNK

    def blocks(t):
        return jnp.moveaxis(t.reshape(bsz, n_blk, Q_BLOCK, *t.shape[2:]), 1, 0)

    def attend_block(inp):
        q_abs_b, q_idx_b, w_idx_b, start = inp
        q_pos = start + jnp.arange(Q_BLOCK)
        q_chunk = q_pos // CHUNK
        allowed = key_chunk[None, :] <= q_chunk[:, None]
        dots = jnp.einsum('bqjd,bsd->bqjs', q_idx_b, k_idx) * (IDX_DIM ** -0.5)
        score = jnp.einsum('bqj,bqjs->bqs', w_idx_b, jax.nn.relu(dots)).astype(f32)
        score = jnp.where(allowed[None], score, -jnp.inf)
        _, idx = lax.top_k(score, k_sel)
        c_sel = jax.vmap(lambda c, ix: c[ix])(c_kv, idx)
        bias = rel_bias[t5_bucket(idx - q_pos[None, :, None])]
        logits = jnp.einsum('bqhr,bqkr->bhqk', q_abs_b, c_sel).astype(f32) * (C_HEAD_DIM ** -0.5) + jnp.moveaxis(bias, -1, 1).astype(f32)
        valid = (idx // CHUNK) <= q_chunk[None, :, None]
        logits = jnp.where(valid[:, None], logits, -jnp.inf)
        probs = jax.nn.softmax(logits, axis=-1).astype(c_sel.dtype)
        return jnp.einsum('bhqk,bqkr->bqhr', probs, c_sel)

    starts = jnp.arange(n_blk, dtype=jnp.int32) * Q_BLOCK
    o_lat = lax.map(attend_block, (blocks(q_abs), blocks(q_idx), blocks(w_idx), starts))
    o_lat = jnp.moveaxis(o_lat, 0, 1).reshape(bsz, seq, C_HEADS, C_KV_RANK)
    return jnp.einsum('bshr,hrd->bshd', o_lat, w_uv).reshape(bsz, seq, C_HEADS * C_HEAD_DIM)


def conv_ffn(x, w_up, conv_w, conv_b, w_down):
    h = causal_dwconv(x @ w_up, conv_w, conv_b)
    gate, up = jnp.split(h, 2, axis=-1)
    return (jax.nn.gelu(gate) * up) @ w_down


def setup_inputs(seed: int = 0) -> dict:
    key = jax.random.key(seed)
    keys = iter(jax.random.split(key, 64))
    f32 = jnp.float32
    E, O, L, D = N_EVEN, N_ODD, DEPTH, D_MODEL

    def nrm(shape, scale):
        return scale * jax.random.normal(next(keys), shape, f32)

    def gain(shape):
        return 1.0 + nrm(shape, 0.02)

    lam_s = jax.random.uniform(next(keys), (E, B_WIDTH), f32, 0.9, 0.999) ** (1.0 / B_C)
    w0 = jnp.linspace(-6.0, -1.0, A_WIDTH, dtype=f32)[None, :] + 0.5 + nrm((E, A_WIDTH), 0.1)
    return {
        "x": nrm((BATCH, SEQ, D), 1.0),
        "p": nrm((DEPTH, BATCH, SEQ, PLE_DIM), 1.0),
        "rel_bias": nrm((REL_BUCKETS, C_HEADS), 0.3),
        "ln1_g": gain((L, D)), "ln1_b": nrm((L, D), 0.02),
        "ln2_g": gain((L, D)), "ln2_b": nrm((L, D), 0.02),
        "ffn_w_up": nrm((L, D, 2 * D_FF), D ** -0.5),
        "ffn_conv_w": nrm((L, FFN_CONV, 2 * D_FF), FFN_CONV ** -0.5),
        "ffn_conv_b": nrm((L, 2 * D_FF), 0.02),
        "ffn_w_down": nrm((L, D_FF, D), DN_BETA * D_FF ** -0.5),
        "ple_w_proj": nrm((L, PLE_DIM, D), PLE_DIM ** -0.5),
        "ple_w_gate": nrm((L, D, D), D ** -0.5),
        "ev_w_in": nrm((E, D, EVEN_COLS), D ** -0.5),
        "ev_w_out": nrm((E, A_WIDTH + B_WIDTH, D), DN_BETA * (A_WIDTH + B_WIDTH) ** -0.5),
        "a_mu": jax.random.uniform(next(keys), (E, A_COLS), f32, 0.2, 0.8),
        "a_w0": w0,
        "a_w2": nrm((E, A_DECAY_LORA, A_WIDTH), 0.1 * A_DECAY_LORA ** -0.5),
        "a_a0": nrm((E, A_WIDTH), 0.1),
        "a_a2": nrm((E, A_ICL_LORA, A_WIDTH), 0.5 * A_ICL_LORA ** -0.5),
        "a_g2": nrm((E, A_GATE_LORA, A_WIDTH), A_GATE_LORA ** -0.5),
        "a_k_k": 0.85 + nrm((E, A_WIDTH), 0.02),
        "a_k_a": gain((E, A_WIDTH)),
        "a_r_k": nrm((E, A_HEADS, A_HEAD), 0.1),
        "a_gn_g": gain((E, A_WIDTH)), "a_gn_b": nrm((E, A_WIDTH), 0.02),
        "b_conv_w": nrm((E, B_CONV, B_WIDTH), B_CONV ** -0.5),
        "b_conv_b": nrm((E, B_WIDTH), 0.02),
        "b_w_r": nrm((E, B_BLOCKS, B_BLOCK, B_BLOCK), B_BLOCK ** -0.5),
        "b_b_r": nrm((E, B_WIDTH), 0.02),
        "b_w_i": nrm((E, B_BLOCKS, B_BLOCK, B_BLOCK), B_BLOCK ** -0.5),
        "b_b_i": nrm((E, B_WIDTH), 0.02),
        "b_lambda": jnp.log(lam_s) - jnp.log1p(-lam_s),
        "od_w_in": nrm((O, D, ODD_COLS), D ** -0.5),
        "od_w_out": nrm((O, C_HEADS * C_HEAD_DIM, D), DN_BETA * (C_HEADS * C_HEAD_DIM) ** -0.5),
        "c_q_norm": gain((O, C_Q_RANK)),
        "c_kv_norm": gain((O, C_KV_RANK)),
        "c_w_uq": nrm((O, C_Q_RANK, C_HEADS * C_HEAD_DIM), C_Q_RANK ** -0.5),
        "c_w_uk": nrm((O, C_HEADS, C_KV_RANK, C_HEAD_DIM), C_KV_RANK ** -0.5),
        "c_w_uv": nrm((O, C_HEADS, C_KV_RANK, C_HEAD_DIM), C_KV_RANK ** -0.5),
        "c_w_qidx": nrm((O, C_Q_RANK, IDX_HEADS * IDX_DIM), C_Q_RANK ** -0.5),
        "c_kidx_g": gain((O, IDX_DIM)), "c_kidx_b": nrm((O, IDX_DIM), 0.02),
    }


def reference(x, p, rel_bias, ln1_g, ln1_b, ln2_g, ln2_b, ffn_w_up, ffn_conv_w, ffn_conv_b, ffn_w_down, ple_w_proj, ple_w_gate, ev_w_in, ev_w_out, a_mu, a_w0, a_w2, a_a0, a_a2, a_g2, a_k_k, a_k_a, a_r_k, a_gn_g, a_gn_b, b_conv_w, b_conv_b, b_w_r, b_b_r, b_w_i, b_b_i, b_lambda, od_w_in, od_w_out, c_q_norm, c_kv_norm, c_w_uq, c_w_uk, c_w_uv, c_w_qidx, c_kidx_g, c_kidx_b):
    for layer in range(DEPTH):
        j = layer // 2
        if layer % 2 == 0:
            z = x @ ev_w_in[j]
            y_a = rwkv7_mix(z[..., :A_COLS], a_mu[j], a_w0[j], a_w2[j], a_a0[j], a_a2[j], a_g2[j], a_k_k[j], a_k_a[j], a_r_k[j], a_gn_g[j], a_gn_b[j])
            y_b = rglru_mix(z[..., A_COLS:], b_conv_w[j], b_conv_b[j], b_w_r[j], b_b_r[j], b_w_i[j], b_b_i[j], b_lambda[j])
            y = jnp.concatenate([y_a, y_b], axis=-1) @ ev_w_out[j]
        else:
            y = dsa_mix(x, od_w_in[j], c_q_norm[j], c_kv_norm[j], c_w_uq[j], c_w_uk[j], c_w_uv[j], c_w_qidx[j], c_kidx_g[j], c_kidx_b[j], rel_bias) @ od_w_out[j]
        x = layer_norm(DN_ALPHA * x + y, ln1_g[layer], ln1_b[layer])
        x = layer_norm(DN_ALPHA * x + conv_ffn(x, ffn_w_up[layer], ffn_conv_w[layer], ffn_conv_b[layer], ffn_w_down[layer]), ln2_g[layer], ln2_b[layer])
        x = x + jax.nn.sigmoid(x @ ple_w_gate[layer]) * (p[layer] @ ple_w_proj[layer])
    return x
```

```python
import numpy as np
from contextlib import ExitStack
import concourse.bass as bass
import concourse.mybir as mybir
from concourse.bass_utils import run_bass_kernel_spmd

F32 = mybir.dt.float32
BF16 = mybir.dt.bfloat16
AF = mybir.ActivationFunctionType
ALU = mybir.AluOpType
AX = mybir.AxisListType


class Buf:
    __slots__ = ("name", "w", "r")

    def __init__(self, name=""):
        self.name = name
        self.w = None
        self.r = {}


class Op:
    __slots__ = ("stream", "emit", "deps", "is_dma", "key", "val", "needs_inc", "snap")


class Prog:
    STREAMS = ("pe", "dve", "act", "pool", "sp")

    def __init__(self, nc):
        self.nc = nc
        self.ops = []
        self.ctx = ExitStack()
        self.E = {"pe": nc.tensor, "dve": nc.vector, "act": nc.scalar, "pool": nc.gpsimd, "sp": nc.sync}
        self.sems = {}
        self.tagcnt = {}
        self.nbuf = 0

    def buf(self, name=""):
        self.nbuf += 1
        return Buf(name or f"b{self.nbuf}")

    def bufs(self, n, name=""):
        return [self.buf(f"{name}{i}") for i in range(n)]

    def sbuf(self, name, shape, dt):
        return self.ctx.enter_context(self.nc.sbuf_tensor("sb_" + name, list(shape), dt))

    def psum(self, name, shape, dt=F32):
        return self.ctx.enter_context(self.nc.psum_tensor("ps_" + name, list(shape), dt))

    def _sem(self, key):
        if key not in self.sems:
            self.sems[key] = self.ctx.enter_context(self.nc.semaphore("s_" + str(key)))
        return self.sems[key]

    def _record(self, stream, emit, reads, writes, is_dma, key):
        op = Op()
        op.stream = stream
        op.emit = emit
        op.is_dma = is_dma
        op.key = key
        op.needs_inc = is_dma
        op.val = 0
        op.snap = None
        deps = []
        for b in reads:
            if b.w is not None:
                deps.append((b.w, True))
        for b in writes:
            if b.w is not None:
                deps.append((b.w, False))
            for r in b.r.values():
                deps.append((r, False))
        op.deps = []
        seen = set()
        for d, raw in deps:
            if d is op or id(d) in seen:
                continue
            if (not is_dma) and (not d.is_dma) and d.stream == stream and not raw:
                continue
            seen.add(id(d))
            d.needs_inc = True
            op.deps.append(d)
        for b in reads:
            b.r[key] = op
        for b in writes:
            b.w = op
            b.r = {}
        self.ops.append(op)
        return op

    def add(self, stream, emit, reads=(), writes=()):
        return self._record(stream, emit, reads, writes, False, stream)

    def dma(self, queue, emit, reads=(), writes=(), tag=None):
        assert tag is not None
        return self._record(queue, emit, reads, writes, True, "d_" + tag)

    def finish(self, final_wait_stream="sp"):
        cnt = {}
        known = {s: {} for s in self.STREAMS}
        nwaits = 0
        last_dma = {}
        for op in self.ops:
            eng = self.E[op.stream]
            kn = known[op.stream]
            for d in op.deps:
                if kn.get(d.key, 0) >= d.val:
                    continue
                eng.wait_ge(self._sem(d.key), d.val)
                nwaits += 1
                for k, v in d.snap.items():
                    if kn.get(k, 0) < v:
                        kn[k] = v
                if kn.get(d.key, 0) < d.val:
                    kn[d.key] = d.val
            inst = op.emit(eng)
            if op.needs_inc:
                inc = 16 if op.is_dma else 1
                cnt[op.key] = cnt.get(op.key, 0) + inc
                op.val = cnt[op.key]
                inst.then_inc(self._sem(op.key), inc)
                op.snap = dict(kn)
                if op.is_dma:
                    last_dma[op.key] = op
        eng = self.E[final_wait_stream]
        for k, op in last_dma.items():
            eng.wait_ge(self._sem(k), op.val)
        self.stats = dict(n_ops=len(self.ops), n_waits=nwaits, n_sems=len(self.sems))
        return self.stats


D = 2048
DFF = 5632
NT = 1026
ALPHA = 8 ** 0.25
LN_EPS = 1e-5
GROUPS = [(0, 2), (2, 514), (514, 1026)]
OWN = [(2, 514), (514, 1026)]


def build_tail(nc):
    def din(name, shape):
        return nc.dram_tensor(name, list(shape), F32, kind="ExternalInput").ap()

    mT = din("mT", [D, NT]); xT = din("xT", [D, NT]); pT = din("pT", [256, 1024])
    flag = din("flag", [128, 1])
    w_out = din("w_out", [D, D]); w_up = din("w_up", [D, 2 * DFF]); w_down = din("w_down", [DFF, D])
    w_pp = din("w_pp", [256, D]); w_pg = din("w_pg", [D, D])
    lnv = din("lnv", [128, 4, 16])
    cwv = din("cwv", [128, 88, 3]); cbv = din("cbv", [128, 88])
    xo = nc.dram_tensor("xoT", [D, 1024], F32, kind="ExternalOutput").ap()
    ts = nc.dram_tensor("ts", [D, NT], F32).ap()
    x1s = nc.dram_tensor("x1s", [D, 1024], F32).ap()
    t2s = nc.dram_tensor("t2s", [D, 1024], F32).ap()
    x2s = nc.dram_tensor("x2s", [D, 1024], F32).ap()

    p = Prog(nc)
    emit_tail(p, nc, dict(mT=mT, xT=xT, pT=pT, flag=flag, w_out=w_out, w_up=w_up, w_down=w_down, w_pp=w_pp,
                          w_pg=w_pg, lnv=lnv, cwv=cwv, cbv=cbv, xo=xo, ts=ts, x1s=x1s, t2s=t2s, x2s=x2s))
    print("tail stats", p.finish())
    return nc


def emit_tail(p, nc, T):
    actT = p.sbuf("actT", [128, 44 * 1024], BF16); actB = p.buf("actT")
    yT = actT[:, 0:16 * NT].rearrange("p (k n) -> p k n", k=16)
    act3 = actT[:].rearrange("p (k n) -> p k n", k=44)
    x1T = p.sbuf("x1T", [128, 16, NT], BF16); x1B = p.buf("x1T")
    ring = [p.sbuf(f"ring{i}", [128, 44, 128], BF16) for i in range(3)]; ringB = p.bufs(3, "ring")
    tmp = [p.sbuf(f"tmp{i}", [128, NT], F32) for i in range(10)]; tmpB = p.bufs(10, "tmp")
    ones = p.sbuf("ones", [128, 128], F32); onesB = p.buf("ones")
    epsb = p.sbuf("epsb", [128, 1], F32)
    lnv = p.sbuf("lnv", [128, 4, 16], F32); cwv = p.sbuf("cwv", [128, 88, 3], F32); cbv = p.sbuf("cbv", [128, 88], F32)
    flag = p.sbuf("flag", [128, 1], F32); cB = p.buf("consts")
    pTs = p.sbuf("pTs", [128, 2, 1024], BF16); pTB = p.buf("pT")
    banks = [p.psum(f"bank{i}", [128, 512]) for i in range(8)]; bankB = p.bufs(8, "bank")

    p.add("dve", lambda e: e.memset(ones[:], 1.0 / D), writes=[onesB])
    p.add("dve", lambda e: e.memset(epsb[:], LN_EPS), writes=[onesB])
    for i, (dst, src) in enumerate([(lnv, T["lnv"]), (cwv, T["cwv"]), (cbv, T["cbv"]), (flag, T["flag"])]):
        p.dma("sp", lambda e, dst=dst, src=src: e.dma_start(out=dst[:], in_=src[:]), writes=[cB], tag="const")

    ring_i = [0]

    def load_w(src_ap, nk):
        s = ring_i[0] % 3
        ring_i[0] += 1
        p.dma("pool", lambda e: e.dma_start(out=ring[s][:, 0:nk, :], in_=src_ap), writes=[ringB[s]], tag=f"ring{s}")
        return s

    def mm_acc(bank, ncols, nk, lhsT_fn, rhs_fn, reads):
        for k in range(nk):
            p.add("pe", lambda e, k=k: e.matmul(banks[bank][:, 0:ncols], lhsT_fn(k), rhs_fn(k),
                                                start=(k == 0), stop=(k == nk - 1)),
                  reads=reads, writes=[bankB[bank]])

    def layer_norm(S, Q, meanb, rstdb, cols, src_dram, gi, bi, dstT, dstB, dst_dram, dst_col0, tl):
        n = cols[-1][1]
        for gidx, (c0, c1) in enumerate(cols):
            p.add("pe", lambda e, c0=c0, c1=c1, gidx=gidx: e.matmul(banks[gidx][:, 0:c1 - c0], ones[:], tmp[S][:, c0:c1], start=True, stop=True),
                  reads=[onesB, tmpB[S]], writes=[bankB[gidx]])
            p.add("act", lambda e, c0=c0, c1=c1, gidx=gidx: e.activation(out=tmp[meanb][:, c0:c1], in_=banks[gidx][:, 0:c1 - c0], func=AF.Identity),
                  reads=[bankB[gidx]], writes=[tmpB[meanb]])
            p.add("pe", lambda e, c0=c0, c1=c1, gidx=gidx: e.matmul(banks[4 + gidx][:, 0:c1 - c0], ones[:], tmp[Q][:, c0:c1], start=True, stop=True),
                  reads=[onesB, tmpB[Q]], writes=[bankB[4 + gidx]])
            p.add("dve", lambda e, c0=c0, c1=c1: e.tensor_tensor(out=tmp[rstdb][:, c0:c1], in0=tmp[meanb][:, c0:c1], in1=tmp[meanb][:, c0:c1], op=ALU.mult),
                  reads=[tmpB[meanb]], writes=[tmpB[rstdb]])
            p.add("dve", lambda e, c0=c0, c1=c1, gidx=gidx: e.tensor_tensor(out=tmp[rstdb][:, c0:c1], in0=banks[4 + gidx][:, 0:c1 - c0], in1=tmp[rstdb][:, c0:c1], op=ALU.subtract),
                  reads=[bankB[4 + gidx], tmpB[rstdb]], writes=[tmpB[rstdb]])
            p.add("act", lambda e, c0=c0, c1=c1: e.activation(out=tmp[rstdb][:, c0:c1], in_=tmp[rstdb][:, c0:c1], func=AF.Sqrt, bias=epsb[:, 0:1]),
                  reads=[tmpB[rstdb], onesB], writes=[tmpB[rstdb]])
            p.add("dve", lambda e, c0=c0, c1=c1: e.reciprocal(out=tmp[rstdb][:, c0:c1], in_=tmp[rstdb][:, c0:c1]),
                  reads=[tmpB[rstdb]], writes=[tmpB[rstdb]])
        for ct in range(16):
            a = tl[ct % 2]
            p.dma("sp", lambda e, ct=ct, a=a: e.dma_start(out=tmp[a][:, 0:n], in_=src_dram[ct * 128:(ct + 1) * 128, 0:n]), writes=[tmpB[a]], tag=f"ln{a}")
            p.add("dve", lambda e, a=a: e.tensor_tensor(out=tmp[a][:, 0:n], in0=tmp[a][:, 0:n], in1=tmp[meanb][:, 0:n], op=ALU.subtract),
                  reads=[tmpB[a], tmpB[meanb]], writes=[tmpB[a]])
            p.add("dve", lambda e, a=a: e.tensor_tensor(out=tmp[a][:, 0:n], in0=tmp[a][:, 0:n], in1=tmp[rstdb][:, 0:n], op=ALU.mult),
                  reads=[tmpB[a], tmpB[rstdb]], writes=[tmpB[a]])
            p.add("act", lambda e, ct=ct, a=a: e.activation(out=dstT[:, ct, 0:n], in_=tmp[a][:, 0:n], func=AF.Identity,
                                                            scale=lnv[:, gi, ct:ct + 1], bias=lnv[:, bi, ct:ct + 1]),
                  reads=[tmpB[a], cB], writes=[dstB])
            p.add("act", lambda e, ct=ct, a=a: e.activation(out=tmp[a][:, 0:n], in_=tmp[a][:, 0:n], func=AF.Identity,
                                                            scale=lnv[:, gi, ct:ct + 1], bias=lnv[:, bi, ct:ct + 1]),
                  reads=[tmpB[a], cB], writes=[tmpB[a]])
            p.dma("sp", lambda e, ct=ct, a=a: e.dma_start(out=dst_dram[ct * 128:(ct + 1) * 128, :], in_=tmp[a][:, dst_col0:n]), reads=[tmpB[a]], tag=f"lnst{a}")

    p.dma("pool", lambda e: e.dma_start(out=yT, in_=T["mT"].rearrange("(k p) n -> p k n", p=128)), writes=[actB], tag="yT")
    S, Q, MEAN, RSTD = 0, 1, 2, 3
    p.add("dve", lambda e: e.memset(tmp[S][:], 0.0), writes=[tmpB[S]])
    p.add("dve", lambda e: e.memset(tmp[Q][:], 0.0), writes=[tmpB[Q]])
    wv = T["w_out"].rearrange("(k p) n -> p k n", p=128)
    for ct in range(16):
        s = load_w(wv[:, :, ct * 128:(ct + 1) * 128], 16)
        xt = 4 + ct % 2; tt = 6 + ct % 2; sq = 8
        p.dma("sp", lambda e, ct=ct, xt=xt: e.dma_start(out=tmp[xt][:], in_=T["xT"][ct * 128:(ct + 1) * 128, :]), writes=[tmpB[xt]], tag=f"xt{xt}")
        for gidx, (c0, c1) in enumerate(GROUPS):
            b = gidx + 3 * (ct % 2)
            mm_acc(b, c1 - c0, 16, lambda k, s=s: ring[s][:, k, :], lambda k, c0=c0, c1=c1: yT[:, k, c0:c1], [ringB[s], actB])
            p.add("dve", lambda e, b=b, c0=c0, c1=c1, xt=xt, tt=tt: e.scalar_tensor_tensor(out=tmp[tt][:, c0:c1], in0=tmp[xt][:, c0:c1], scalar=ALPHA, in1=banks[b][:, 0:c1 - c0], op0=ALU.mult, op1=ALU.add),
                  reads=[tmpB[xt], bankB[b]], writes=[tmpB[tt]])
        p.add("act", lambda e, tt=tt: e.activation(out=tmp[sq][:], in_=tmp[tt][:], func=AF.Square), reads=[tmpB[tt]], writes=[tmpB[sq]])
        p.add("dve", lambda e, tt=tt: e.tensor_tensor(out=tmp[S][:], in0=tmp[S][:], in1=tmp[tt][:], op=ALU.add), reads=[tmpB[S], tmpB[tt]], writes=[tmpB[S]])
        p.add("dve", lambda e: e.tensor_tensor(out=tmp[Q][:], in0=tmp[Q][:], in1=tmp[sq][:], op=ALU.add), reads=[tmpB[Q], tmpB[sq]], writes=[tmpB[Q]])
        p.dma("sp", lambda e, ct=ct, tt=tt: e.dma_start(out=T["ts"][ct * 128:(ct + 1) * 128, :], in_=tmp[tt][:]), reads=[tmpB[tt]], tag=f"ts{tt}")
    layer_norm(S, Q, MEAN, RSTD, GROUPS, T["ts"], 0, 1, x1T, x1B, T["x1s"], 2, (4, 5))

    wv = T["w_up"].rearrange("(k p) n -> p k n", p=128)
    for i in range(44):
        cvs = []
        for half in range(2):
            ctile = i + 44 * half
            s = load_w(wv[:, :, ctile * 128:(ctile + 1) * 128], 16)
            hb = (i % 2) * 2 + half
            for gidx, (c0, c1) in enumerate(GROUPS):
                b = gidx + 3 * (ring_i[0] % 2)
                mm_acc(b, c1 - c0, 16, lambda k, s=s: ring[s][:, k, :], lambda k, c0=c0, c1=c1: x1T[:, k, c0:c1], [ringB[s], x1B])
                if gidx == 0:
                    p.add("act", lambda e, b=b, hb=hb: e.activation(out=tmp[hb][:, 0:2], in_=banks[b][:, 0:2], func=AF.Identity, scale=flag[:, 0:1]),
                          reads=[bankB[b], cB], writes=[tmpB[hb]])
                else:
                    p.add("act", lambda e, b=b, hb=hb, c0=c0, c1=c1: e.activation(out=tmp[hb][:, c0:c1], in_=banks[b][:, 0:c1 - c0], func=AF.Identity),
                          reads=[bankB[b]], writes=[tmpB[hb]])
            cv = 4 + (i % 2) * 2 + half
            p.add("dve", lambda e, hb=hb, cv=cv, ctile=ctile: e.tensor_scalar(out=tmp[cv][:, 0:1024], in0=tmp[hb][:, 2:1026], scalar1=cwv[:, ctile, 2:3], scalar2=cbv[:, ctile:ctile + 1], op0=ALU.mult, op1=ALU.add),
                  reads=[tmpB[hb], cB], writes=[tmpB[cv]])
            for j, off in ((1, 1), (0, 0)):
                p.add("dve", lambda e, hb=hb, cv=cv, ctile=ctile, j=j, off=off: e.scalar_tensor_tensor(out=tmp[cv][:, 0:1024], in0=tmp[hb][:, off:off + 1024], scalar=cwv[:, ctile, j:j + 1], in1=tmp[cv][:, 0:1024], op0=ALU.mult, op1=ALU.add),
                      reads=[tmpB[hb], tmpB[cv], cB], writes=[tmpB[cv]])
            cvs.append(cv)
        gl = 8 + i % 2
        p.add("act", lambda e, gl=gl, cg=cvs[0]: e.activation(out=tmp[gl][:, 0:1024], in_=tmp[cg][:, 0:1024], func=AF.Gelu_apprx_tanh), reads=[tmpB[cvs[0]]], writes=[tmpB[gl]])
        p.add("dve", lambda e, gl=gl, cu=cvs[1], i=i: e.tensor_tensor(out=act3[:, i, :], in0=tmp[gl][:, 0:1024], in1=tmp[cu][:, 0:1024], op=ALU.mult),
              reads=[tmpB[gl], tmpB[cvs[1]]], writes=[actB])

    C2 = [(0, 512), (512, 1024)]
    p.add("dve", lambda e: e.memset(tmp[S][:], 0.0), writes=[tmpB[S]])
    p.add("dve", lambda e: e.memset(tmp[Q][:], 0.0), writes=[tmpB[Q]])
    wv = T["w_down"].rearrange("(k p) n -> p k n", p=128)
    for ct in range(16):
        s = load_w(wv[:, :, ct * 128:(ct + 1) * 128], 44)
        xt = 4 + ct % 2; tt = 6 + ct % 2; sq = 8
        p.dma("sp", lambda e, ct=ct, xt=xt: e.dma_start(out=tmp[xt][:, 0:1024], in_=T["x1s"][ct * 128:(ct + 1) * 128, :]), writes=[tmpB[xt]], tag=f"xt{xt}")
        for gidx, (c0, c1) in enumerate(C2):
            b = gidx + 3 * (ct % 2)
            mm_acc(b, 512, 44, lambda k, s=s: ring[s][:, k, :], lambda k, c0=c0, c1=c1: act3[:, k, c0:c1], [ringB[s], actB])
            p.add("dve", lambda e, b=b, c0=c0, c1=c1, xt=xt, tt=tt: e.scalar_tensor_tensor(out=tmp[tt][:, c0:c1], in0=tmp[xt][:, c0:c1], scalar=ALPHA, in1=banks[b][:, 0:512], op0=ALU.mult, op1=ALU.add),
                  reads=[tmpB[xt], bankB[b]], writes=[tmpB[tt]])
        p.add("act", lambda e, tt=tt: e.activation(out=tmp[sq][:, 0:1024], in_=tmp[tt][:, 0:1024], func=AF.Square), reads=[tmpB[tt]], writes=[tmpB[sq]])
        p.add("dve", lambda e, tt=tt: e.tensor_tensor(out=tmp[S][:, 0:1024], in0=tmp[S][:, 0:1024], in1=tmp[tt][:, 0:1024], op=ALU.add), reads=[tmpB[S], tmpB[tt]], writes=[tmpB[S]])
        p.add("dve", lambda e: e.tensor_tensor(out=tmp[Q][:, 0:1024], in0=tmp[Q][:, 0:1024], in1=tmp[sq][:, 0:1024], op=ALU.add), reads=[tmpB[Q], tmpB[sq]], writes=[tmpB[Q]])
        p.dma("sp", lambda e, ct=ct, tt=tt: e.dma_start(out=T["t2s"][ct * 128:(ct + 1) * 128, :], in_=tmp[tt][:, 0:1024]), reads=[tmpB[tt]], tag=f"ts{tt}")
    layer_norm(S, Q, MEAN, RSTD, C2, T["t2s"], 2, 3, x1T, x1B, T["x2s"], 0, (4, 5))

    p.dma("pool", lambda e: e.dma_start(out=pTs[:], in_=T["pT"].rearrange("(k p) n -> p k n", p=128)), writes=[pTB], tag="pT")
    wg = T["w_pg"].rearrange("(k p) n -> p k n", p=128)
    wp = T["w_pp"].rearrange("(k p) n -> p k n", p=128)
    for ct in range(16):
        sg_ = load_w(wg[:, :, ct * 128:(ct + 1) * 128], 16)
        sp_ = load_w(wp[:, :, ct * 128:(ct + 1) * 128], 2)
        x2 = 4 + ct % 2; sg = 6 + ct % 2; ot = 8 + ct % 2
        p.dma("sp", lambda e, ct=ct, x2=x2: e.dma_start(out=tmp[x2][:, 0:1024], in_=T["x2s"][ct * 128:(ct + 1) * 128, :]), writes=[tmpB[x2]], tag=f"xt{x2}")
        for gidx, (c0, c1) in enumerate(C2):
            b = gidx + 4 * (ct % 2)
            mm_acc(b, 512, 16, lambda k, s=sg_: ring[s][:, k, :], lambda k, c0=c0, c1=c1: x1T[:, k, c0:c1], [ringB[sg_], x1B])
            mm_acc(b + 2, 512, 2, lambda k, s=sp_: ring[s][:, k, :], lambda k, c0=c0, c1=c1: pTs[:, k, c0:c1], [ringB[sp_], pTB])
            p.add("act", lambda e, b=b, sg=sg, c0=c0, c1=c1: e.activation(out=tmp[sg][:, c0:c1], in_=banks[b][:, 0:512], func=AF.Sigmoid), reads=[bankB[b]], writes=[tmpB[sg]])
            p.add("dve", lambda e, b=b, sg=sg, c0=c0, c1=c1: e.tensor_tensor(out=tmp[sg][:, c0:c1], in0=tmp[sg][:, c0:c1], in1=banks[b + 2][:, 0:512], op=ALU.mult),
                  reads=[tmpB[sg], bankB[b + 2]], writes=[tmpB[sg]])
        p.add("dve", lambda e, sg=sg, x2=x2, ot=ot: e.tensor_tensor(out=tmp[ot][:, 0:1024], in0=tmp[sg][:, 0:1024], in1=tmp[x2][:, 0:1024], op=ALU.add),
              reads=[tmpB[sg], tmpB[x2]], writes=[tmpB[ot]])
        p.dma("sp", lambda e, ct=ct, ot=ot: e.dma_start(out=T["xo"][ct * 128:(ct + 1) * 128, :], in_=tmp[ot][:, 0:1024]), reads=[tmpB[ot]], tag=f"out{ot}")


D = 2048
NCOL = 928
COLT = [(0, 128), (128, 128), (256, 128), (384, 128), (512, 128), (640, 32), (672, 128), (800, 128)]
ST = 512
CH = 64


def build_even(nc, seq=8192):
    def din(name, shape):
        return nc.dram_tensor(name, list(shape), F32, kind="ExternalInput").ap()
    T = dict(xT=din("xT", [D, seq]), w_in=din("w_in", [D, NCOL]), mu=din("mu", [128, 8]), rp=din("rp", [128, 8]),
             lw=din("lw", [128, 128]), g2a=din("g2a", [128, 128]), g2b=din("g2b", [32, 128]), bp=din("bp", [128, 8]),
             wr=din("wr", [128, 128]), wi=din("wi", [128, 128]), cm=din("cm", [128, 6, 128]), chm=din("chm", [128, ST]))
    T["yT"] = nc.dram_tensor("yT", [256, seq], F32, kind="ExternalOutput").ap()
    p = Prog(nc)
    emit_even(p, nc, T, seq)
    print("even stats", p.finish())
    return nc


def emit_even(p, nc, T, seq):
    NS = seq // ST
    xt = p.sbuf("xt", [128, 16, ST], BF16); xtB = p.buf()
    w_in = p.sbuf("w_in", [128, 16, NCOL], BF16); wB = p.buf()
    mu = p.sbuf("mu", [128, 8], F32); rp = p.sbuf("rp", [128, 8], F32); bp = p.sbuf("bp", [128, 8], F32)
    prm = p.sbuf("prm", [128, 4], F32)
    lw = p.sbuf("lw", [128, 128], F32); g2a = p.sbuf("g2a", [128, 128], F32); g2b = p.sbuf("g2b", [32, 128], F32)
    wr = p.sbuf("wr", [128, 128], F32); wi = p.sbuf("wi", [128, 128], F32)
    cm = p.sbuf("cm", [128, 6, 128], F32); chm = p.sbuf("chm", [128, ST], F32)
    cB = p.buf("consts")
    I_, BLK, NSU, UI, NSL, SU = (cm[:, i, :] for i in range(6))
    z = [p.sbuf(f"z{i}", [128, ST + 1], F32) for i in range(6)]; zB = p.bufs(6, "z")
    zxb = p.sbuf("zxb", [128, ST + 3], F32); zxbB = p.buf()
    NTMP = 22
    t_ = [p.sbuf(f"e{i}", [128, ST], F32) for i in range(NTMP)]; tB = p.bufs(NTMP, "e")
    KR = p.sbuf("KR", [128, 8, 2, 128], F32); KRB = p.buf()
    Bb = p.sbuf("Bb", [128, 8, 128], F32); BbB = p.buf()
    Kb = p.sbuf("Kb", [128, 8, 128], F32); KbB = p.buf()
    Vb = p.sbuf("Vb", [128, 8, 128], F32); VbB = p.buf()
    NM = 16
    m_ = [p.sbuf(f"m{i}", [128, 4, 128], F32) for i in range(NM)]; mB = p.bufs(NM, "m")
    KAV = p.sbuf("KAV", [128, 4, 256], F32); KAVB = p.buf()
    TKW = p.sbuf("TKW", [128, 4, 256], F32); TKWB = p.buf()
    Hs = p.sbuf("Hs", [128, 9, 128], F32); HsB = p.bufs(9, "Hs")
    Ysb = p.sbuf("Ysb", [128, 8, 128], F32); YsbB = p.buf()
    gs = p.sbuf("gs", [128, 8, 4], F32); gsB = p.buf()
    hst = p.sbuf("hst", [128, 1], F32); hstB = p.buf()
    banks = [p.psum(f"bank{i}", [128, 512]) for i in range(8)]; bankB = p.bufs(8, "bank")

    def dve(fn, R, W): p.add("dve", fn, reads=R, writes=W)
    def act(fn, R, W): p.add("act", fn, reads=R, writes=W)
    def pe(fn, R, W): p.add("pe", fn, reads=R, writes=W)

    for dst, src in [(mu, "mu"), (rp, "rp"), (bp, "bp"), (lw, "lw"), (g2a, "g2a"), (g2b, "g2b"), (wr, "wr"), (wi, "wi"), (cm, "cm"), (chm, "chm")]:
        p.dma("sp", lambda e, dst=dst, src=src: e.dma_start(out=dst[:], in_=T[src][:]), writes=[cB], tag="const")
    p.dma("pool", lambda e: e.dma_start(out=w_in[:], in_=T["w_in"].rearrange("(k p) n -> p k n", p=128)), writes=[wB], tag="w_in")
    W0, A0, KK, KA, GNG, GNB, RK = (rp[:, i:i + 1] for i in range(7))
    CW = [bp[:, i:i + 1] for i in range(4)]; CBI, BR, BI, LAM = (bp[:, i:i + 1] for i in range(4, 8))
    OMK, SP8, SP16 = (prm[:, i:i + 1] for i in range(3))
    dve(lambda e: e.tensor_scalar(out=OMK, in0=KA, scalar1=-1.0, scalar2=1.0, op0=ALU.mult, op1=ALU.add), [cB], [cB])
    act(lambda e: e.activation(out=prm[:, 3:4], in_=LAM, func=AF.Exp, scale=-1.0), [cB], [cB])
    act(lambda e: e.activation(out=prm[:, 3:4], in_=prm[:, 3:4], func=AF.Ln, bias=1.0), [cB], [cB])
    dve(lambda e: e.tensor_scalar(out=SP8, in0=prm[:, 3:4], scalar1=-8.0, scalar2=None, op0=ALU.mult), [cB], [cB])
    dve(lambda e: e.tensor_scalar(out=SP16, in0=prm[:, 3:4], scalar1=-16.0, scalar2=None, op0=ALU.mult), [cB], [cB])
    for zz, zb in list(zip(z, zB)) + [(zxb, zxbB)]:
        dve(lambda e, zz=zz: e.memset(zz[:], 0.0), [], [zb])
    for tile_, b_ in [(KR, KRB), (Bb, BbB), (Kb, KbB), (Vb, VbB), (Hs, HsB[0]), (hst, hstB)]:
        dve(lambda e, tile_=tile_: e.memset(tile_[:], 0.0), [], [b_])

    xv = T["xT"].rearrange("(k p) n -> p k n", p=128)
    bank_rr = [0]

    def nb():
        bank_rr[0] = (bank_rr[0] + 1) % 8
        return bank_rr[0]

    for s in range(NS):
        t0 = s * ST
        p.dma("pool", lambda e, t0=t0: e.dma_start(out=xt[:], in_=xv[:, :, t0:t0 + ST]), writes=[xtB], tag="xt")
        for ci, (c0, cn) in enumerate(COLT):
            b = nb()
            for k in range(16):
                pe(lambda e, b=b, k=k, c0=c0, cn=cn: e.matmul(banks[b][0:cn, :], w_in[:, k, c0:c0 + cn], xt[:, k, :], start=(k == 0), stop=(k == 15)),
                   [wB, xtB], [bankB[b]])
            if ci < 6:
                act(lambda e, b=b, ci=ci, cn=cn: e.activation(out=z[ci][0:cn, 1:ST + 1], in_=banks[b][0:cn, :], func=AF.Identity), [bankB[b]], [zB[ci]])
            elif ci == 6:
                act(lambda e, b=b: e.activation(out=zxb[:, 3:ST + 3], in_=banks[b][:, :], func=AF.Identity), [bankB[b]], [zxbB])
            else:
                GATE = 21
                act(lambda e, b=b: e.activation(out=t_[GATE][:], in_=banks[b][:, :], func=AF.Gelu_apprx_tanh), [bankB[b]], [tB[GATE]])
        ZT = [0, 1, 2, 3, 4, 5]
        for ci in range(6):
            cn = COLT[ci][1]
            dve(lambda e, ci=ci, cn=cn: e.tensor_tensor(out=t_[ZT[ci]][0:cn, :], in0=z[ci][0:cn, 0:ST], in1=z[ci][0:cn, 1:ST + 1], op=ALU.subtract), [zB[ci]], [tB[ZT[ci]]])
            dve(lambda e, ci=ci, cn=cn: e.scalar_tensor_tensor(out=t_[ZT[ci]][0:cn, :], in0=t_[ZT[ci]][0:cn, :], scalar=mu[0:cn, ci:ci + 1], in1=z[ci][0:cn, 1:ST + 1], op0=ALU.mult, op1=ALU.add),
                [tB[ZT[ci]], zB[ci], cB], [tB[ZT[ci]]])
            act(lambda e, ci=ci, cn=cn: e.activation(out=z[ci][0:cn, 0:1], in_=z[ci][0:cn, ST:ST + 1], func=AF.Identity), [zB[ci]], [zB[ci]])
        R_, K_, V_, L_, G1, G2 = ZT
        TH, LD, AA, GG, KKt, KP, BB_, CUM, E1, E2, E3, BON, TMP = 6, 7, 8, 9, 10, 11, 12, 13, 14, 15, 16, 17, 18
        act(lambda e: e.activation(out=t_[TH][0:64, :], in_=t_[L_][0:64, :], func=AF.Tanh), [tB[L_]], [tB[TH]])
        b = nb()
        pe(lambda e, b=b: e.matmul(banks[b][:, :], lw[0:64, :], t_[TH][0:64, :], start=True, stop=True), [cB, tB[TH]], [bankB[b]])
        act(lambda e, b=b: e.activation(out=t_[LD][:], in_=banks[b][:, :], func=AF.Sigmoid, bias=W0), [bankB[b], cB], [tB[LD]])
        dve(lambda e: e.tensor_scalar(out=t_[LD][:], in0=t_[LD][:], scalar1=-0.6065306597126334, scalar2=None, op0=ALU.mult), [tB[LD]], [tB[LD]])
        b = nb()
        pe(lambda e, b=b: e.matmul(banks[b][:, :], lw[64:128, :], t_[L_][64:128, :], start=True, stop=True), [cB, tB[L_]], [bankB[b]])
        act(lambda e, b=b: e.activation(out=t_[AA][:], in_=banks[b][:, :], func=AF.Sigmoid, bias=A0), [bankB[b], cB], [tB[AA]])
        act(lambda e: e.activation(out=t_[G1][:], in_=t_[G1][:], func=AF.Sigmoid), [tB[G1]], [tB[G1]])
        act(lambda e: e.activation(out=t_[G2][0:32, :], in_=t_[G2][0:32, :], func=AF.Sigmoid), [tB[G2]], [tB[G2]])
        b = nb()
        pe(lambda e, b=b: e.matmul(banks[b][:, :], g2a[:], t_[G1][:], start=True, stop=False), [cB, tB[G1]], [bankB[b]])
        pe(lambda e, b=b: e.matmul(banks[b][:, :], g2b[:], t_[G2][0:32, :], start=False, stop=True), [cB, tB[G2]], [bankB[b]])
        act(lambda e, b=b: e.activation(out=t_[GG][:], in_=banks[b][:, :], func=AF.Identity), [bankB[b]], [tB[GG]])
        dve(lambda e: e.tensor_scalar(out=t_[KKt][:], in0=t_[K_][:], scalar1=KK, scalar2=None, op0=ALU.mult), [tB[K_], cB], [tB[KKt]])
        act(lambda e: e.activation(out=t_[TMP][:], in_=t_[KKt][:], func=AF.Square), [tB[KKt]], [tB[TMP]])
        b = nb()
        pe(lambda e, b=b: e.matmul(banks[b][:, :], BLK, t_[TMP][:], start=True, stop=True), [cB, tB[TMP]], [bankB[b]])
        act(lambda e, b=b: e.activation(out=t_[TMP][:], in_=banks[b][:, :], func=AF.Sqrt), [bankB[b]], [tB[TMP]])
        dve(lambda e: e.tensor_scalar(out=t_[TMP][:], in0=t_[TMP][:], scalar1=1e-12, scalar2=None, op0=ALU.max), [tB[TMP]], [tB[TMP]])
        dve(lambda e: e.reciprocal(out=t_[TMP][:], in_=t_[TMP][:]), [tB[TMP]], [tB[TMP]])
        dve(lambda e: e.tensor_tensor(out=t_[KKt][:], in0=t_[KKt][:], in1=t_[TMP][:], op=ALU.mult), [tB[KKt], tB[TMP]], [tB[KKt]])
        dve(lambda e: e.tensor_scalar(out=t_[KP][:], in0=t_[AA][:], scalar1=KA, scalar2=OMK, op0=ALU.mult, op1=ALU.add), [tB[AA], cB], [tB[KP]])
        dve(lambda e: e.tensor_tensor(out=t_[KP][:], in0=t_[KP][:], in1=t_[K_][:], op=ALU.mult), [tB[KP], tB[K_]], [tB[KP]])
        dve(lambda e: e.tensor_tensor(out=t_[BB_][:], in0=t_[AA][:], in1=t_[KKt][:], op=ALU.mult), [tB[AA], tB[KKt]], [tB[BB_]])
        dve(lambda e: e.scalar_tensor_tensor(out=t_[TMP][:], in0=t_[R_][:], scalar=RK, in1=t_[KP][:], op0=ALU.mult, op1=ALU.mult), [tB[R_], tB[KP], cB], [tB[TMP]])
        b = nb()
        pe(lambda e, b=b: e.matmul(banks[b][:, :], BLK, t_[TMP][:], start=True, stop=True), [cB, tB[TMP]], [bankB[b]])
        dve(lambda e, b=b: e.tensor_tensor(out=t_[BON][:], in0=banks[b][:, :], in1=t_[V_][:], op=ALU.mult), [bankB[b], tB[V_]], [tB[BON]])
        dve(lambda e: e.tensor_tensor_scan(out=t_[CUM][:], data0=chm[:], data1=t_[LD][:], initial=0.0, op0=ALU.mult, op1=ALU.add), [cB, tB[LD]], [tB[CUM]])
        act(lambda e: e.activation(out=t_[E1][:], in_=t_[CUM][:], func=AF.Exp), [tB[CUM]], [tB[E1]])
        act(lambda e: e.activation(out=t_[E3][:], in_=t_[CUM][:], func=AF.Exp, scale=-1.0), [tB[CUM]], [tB[E3]])
        dve(lambda e: e.tensor_tensor(out=t_[CUM][:], in0=t_[CUM][:], in1=t_[LD][:], op=ALU.subtract), [tB[CUM], tB[LD]], [tB[CUM]])
        act(lambda e: e.activation(out=t_[E2][:], in_=t_[CUM][:], func=AF.Exp), [tB[CUM]], [tB[E2]])
        def to_bd(dst4, a_idx, b_idx, W):
            for h in range(2):
                ps = slice(64 * h, 64 * h + 64)
                if b_idx is None:
                    dve(lambda e, ps=ps, h=h: e.tensor_copy(out=dst4[ps, :, 64 * h:64 * h + 64], in_=t_[a_idx][ps, :].rearrange("p (c t) -> p c t", t=CH)), [tB[a_idx]], [W])
                else:
                    dve(lambda e, ps=ps, h=h: e.tensor_tensor(out=dst4[ps, :, 64 * h:64 * h + 64], in0=t_[a_idx][ps, :].rearrange("p (c t) -> p c t", t=CH),
                                                              in1=t_[b_idx][ps, :].rearrange("p (c t) -> p c t", t=CH), op=ALU.mult), [tB[a_idx], tB[b_idx]], [W])
        to_bd(KR[:, :, 0, :], KKt, E2, KRB)
        to_bd(KR[:, :, 1, :], R_, E1, KRB)
        to_bd(Bb[:], BB_, E3, BbB)
        to_bd(Kb[:], KP, E3, KbB)
        to_bd(Vb[:], V_, None, VbB)

        for hb in range(2):
            cs = slice(4 * hb, 4 * hb + 4)
            MT, MM, QT, AKT, LBT, LKT, KTt, BTt, KHt, VTt, PHI, PSI, OMT, Y0 = range(14)
            b0, b1 = nb(), nb()
            for j in range(4):
                c = 4 * hb + j
                bb = b0 if j < 2 else b1
                pe(lambda e, bb=bb, c=c, j=j: e.matmul(banks[bb][:, (j % 2) * 256:(j % 2) * 256 + 256], Bb[:, c, :], KR[:, c, :, :].rearrange("p a b -> p (a b)"), start=True, stop=True),
                   [BbB, KRB], [bankB[bb]])
            for half, bb in enumerate((b0, b1)):
                v = banks[bb][:, :].rearrange("p (j x) -> p j x", x=256)
                dve(lambda e, v=v, half=half: e.tensor_tensor(out=m_[MT][:, 2 * half:2 * half + 2, :], in0=v[:, :, 0:128], in1=NSU.unsqueeze(1).to_broadcast([128, 2, 128]), op=ALU.mult), [bankB[bb], cB], [mB[MT]])
                dve(lambda e, v=v, half=half: e.tensor_tensor(out=m_[LBT][:, 2 * half:2 * half + 2, :], in0=v[:, :, 128:256], in1=UI.unsqueeze(1).to_broadcast([128, 2, 128]), op=ALU.mult), [bankB[bb], cB], [mB[LBT]])
            b0, b1 = nb(), nb()
            for j in range(4):
                c = 4 * hb + j
                bb = b0 if j < 2 else b1
                pe(lambda e, bb=bb, c=c, j=j: e.matmul(banks[bb][:, (j % 2) * 256:(j % 2) * 256 + 256], Kb[:, c, :], KR[:, c, :, :].rearrange("p a b -> p (a b)"), start=True, stop=True),
                   [KbB, KRB], [bankB[bb]])
            for half, bb in enumerate((b0, b1)):
                v = banks[bb][:, :].rearrange("p (j x) -> p j x", x=256)
                dve(lambda e, v=v, half=half: e.tensor_tensor(out=m_[AKT][:, 2 * half:2 * half + 2, :], in0=v[:, :, 0:128], in1=SU.unsqueeze(1).to_broadcast([128, 2, 128]), op=ALU.mult), [bankB[bb], cB], [mB[AKT]])
                dve(lambda e, v=v, half=half: e.tensor_tensor(out=m_[LKT][:, 2 * half:2 * half + 2, :], in0=v[:, :, 128:256], in1=UI.unsqueeze(1).to_broadcast([128, 2, 128]), op=ALU.mult), [bankB[bb], cB], [mB[LKT]])
            bb = nb()
            for j in range(4):
                c = 4 * hb + j
                pe(lambda e, bb=bb, c=c, j=j: e.matmul(banks[bb][:, j * 128:(j + 1) * 128], KR[:, c, 0, :], Bb[:, c, :], start=True, stop=True), [KRB, BbB], [bankB[bb]])
            dve(lambda e, bb=bb: e.tensor_tensor(out=m_[MM][:], in0=banks[bb][:, :].rearrange("p (j x) -> p j x", x=128), in1=NSL.unsqueeze(1).to_broadcast([128, 4, 128]), op=ALU.mult), [bankB[bb], cB], [mB[MM]])
            dve(lambda e: e.tensor_tensor(out=m_[QT][:], in0=m_[MT][:], in1=I_.unsqueeze(1).to_broadcast([128, 4, 128]), op=ALU.add), [mB[MT], cB], [mB[QT]])
            cur_mt, cur_mm, cur_qt = MT, MM, QT
            alt = [14, 15]
            spare = [PHI, PSI, OMT, Y0]
            for lvl in range(5):
                n_mt, n_mm, n_qt = spare[0], spare[1], spare[2]
                bq = nb()
                if lvl < 4:
                    bu = nb()
                    for j in range(4):
                        pe(lambda e, bu=bu, j=j, a=cur_mm, b_=cur_mt: e.matmul(banks[bu][:, j * 128:(j + 1) * 128], m_[a][:, j, :], m_[b_][:, j, :], start=True, stop=True), [mB[cur_mm], mB[cur_mt]], [bankB[bu]])
                bw = nb()
                for j in range(4):
                    pe(lambda e, bw=bw, j=j, a=cur_mt, b_=cur_mm: e.matmul(banks[bw][:, j * 128:(j + 1) * 128], m_[a][:, j, :], m_[b_][:, j, :], start=True, stop=True), [mB[cur_mm], mB[cur_mt]], [bankB[bw]])
                if lvl < 4:
                    act(lambda e, bu=bu, n_mt=n_mt: e.activation(out=m_[n_mt][:].rearrange("p j x -> p (j x)"), in_=banks[bu][:, :], func=AF.Identity), [bankB[bu]], [mB[n_mt]])
                dve(lambda e, bw=bw, n_mm=n_mm: e.tensor_copy(out=m_[n_mm][:].rearrange("p j x -> p (j x)"), in_=banks[bw][:, :]), [bankB[bw]], [mB[n_mm]])
                for j in range(4):
                    pe(lambda e, bq=bq, j=j, q=cur_qt: e.matmul(banks[bq][:, j * 128:(j + 1) * 128], I_, m_[q][:, j, :], start=True, stop=False), [cB, mB[cur_qt]], [bankB[bq]])
                    pe(lambda e, bq=bq, j=j, q=cur_qt, n_mm=n_mm: e.matmul(banks[bq][:, j * 128:(j + 1) * 128], m_[n_mm][:, j, :], m_[q][:, j, :], start=False, stop=True), [mB[n_mm], mB[cur_qt]], [bankB[bq]])
                act(lambda e, bq=bq, n_qt=n_qt: e.activation(out=m_[n_qt][:].rearrange("p j x -> p (j x)"), in_=banks[bq][:, :], func=AF.Identity), [bankB[bq]], [mB[n_qt]])
                spare = [cur_mt, cur_mm, cur_qt] + spare[3:]
                cur_mt, cur_mm, cur_qt = n_mt, n_mm, n_qt
            QTF = cur_qt
            free_ = [i for i in (MT, MM, QT, PHI, PSI, OMT, Y0, 14, 15) if i != QTF]
            PHI, PSI, OMT, Y0, X1, X2 = free_[:6]
            for src3, dsti, srcB in ((KR[:, :, 0, :], KTt, KRB), (Bb[:], BTt, BbB), (Kb[:], KHt, KbB), (Vb[:], VTt, VbB)):
                bb = nb()
                for j in range(4):
                    c = 4 * hb + j
                    pe(lambda e, bb=bb, j=j, c=c, src3=src3: e.transpose(banks[bb][:, j * 128:(j + 1) * 128], src3[:, c, :], I_), [srcB, cB], [bankB[bb]])
                if dsti == KTt:
                    act(lambda e, bb=bb: e.activation(out=KAV[:, :, 0:128], in_=banks[bb][:, :].rearrange("p (j x) -> p j x", x=128), func=AF.Identity), [bankB[bb]], [KAVB])
                else:
                    act(lambda e, bb=bb, dsti=dsti: e.activation(out=m_[dsti][:].rearrange("p j x -> p (j x)"), in_=banks[bb][:, :], func=AF.Identity), [bankB[bb]], [mB[dsti]])
            bb = nb()
            for j in range(4):
                pe(lambda e, bb=bb, j=j: e.matmul(banks[bb][:, j * 128:(j + 1) * 128], m_[AKT][:, j, :], m_[VTt][:, j, :], start=True, stop=True), [mB[AKT], mB[VTt]], [bankB[bb]])
            dve(lambda e, bb=bb: e.tensor_copy(out=KAV[:, :, 128:256], in_=banks[bb][:, :].rearrange("p (j x) -> p j x", x=128)), [bankB[bb]], [KAVB])
            b0, b1 = nb(), nb()
            for j in range(4):
                bb = b0 if j < 2 else b1
                pe(lambda e, bb=bb, j=j: e.matmul(banks[bb][:, (j % 2) * 256:(j % 2) * 256 + 256], m_[QTF][:, j, :], KAV[:, j, :], start=True, stop=True), [mB[QTF], KAVB], [bankB[bb]])
            for half, bb in enumerate((b0, b1)):
                v = banks[bb][:, :].rearrange("p (j x) -> p j x", x=256)
                act(lambda e, v=v, half=half: e.activation(out=TKW[:, 2 * half:2 * half + 2, 0:128], in_=v[:, :, 0:128], func=AF.Identity), [bankB[bb]], [TKWB])
                dve(lambda e, v=v, half=half: e.tensor_scalar(out=TKW[:, 2 * half:2 * half + 2, 128:256], in0=v[:, :, 128:256], scalar1=-1.0, scalar2=None, op0=ALU.mult), [bankB[bb]], [TKWB])
            bb = nb()
            for j in range(4):
                pe(lambda e, bb=bb, j=j: e.matmul(banks[bb][:, j * 128:(j + 1) * 128], TKW[:, j, 0:128], m_[BTt][:, j, :], start=True, stop=True), [TKWB, mB[BTt]], [bankB[bb]])
            dve(lambda e, bb=bb, PHI=PHI: e.tensor_tensor(out=m_[PHI][:], in0=I_.unsqueeze(1).to_broadcast([128, 4, 128]), in1=banks[bb][:, :].rearrange("p (j x) -> p j x", x=128), op=ALU.subtract), [bankB[bb], cB], [mB[PHI]])
            bb = nb()
            for j in range(4):
                pe(lambda e, bb=bb, j=j: e.matmul(banks[bb][:, j * 128:(j + 1) * 128], m_[KHt][:, j, :], m_[VTt][:, j, :], start=True, stop=False), [mB[KHt], mB[VTt]], [bankB[bb]])
                pe(lambda e, bb=bb, j=j: e.matmul(banks[bb][:, j * 128:(j + 1) * 128], m_[BTt][:, j, :], TKW[:, j, 128:256], start=False, stop=True), [mB[BTt], TKWB], [bankB[bb]])
            for j in range(4):
                c = 4 * hb + j
                act(lambda e, bb=bb, j=j, c=c, PSI=PSI: e.activation(out=m_[PSI][:, j, :], in_=banks[bb][:, j * 128:(j + 1) * 128], func=AF.Identity, scale=t_[E1][:, c * CH + CH - 1:c * CH + CH]),
                    [bankB[bb], tB[E1]], [mB[PSI]])
            bb = nb()
            for j in range(4):
                pe(lambda e, bb=bb, j=j: e.matmul(banks[bb][:, j * 128:(j + 1) * 128], TKW[:, j, 0:128], m_[LBT][:, j, :], start=True, stop=True), [TKWB, mB[LBT]], [bankB[bb]])
            dve(lambda e, bb=bb, OMT=OMT, cs=cs: e.tensor_tensor(out=m_[OMT][:], in0=KR[:, cs, 1, :], in1=banks[bb][:, :].rearrange("p (j x) -> p j x", x=128), op=ALU.subtract), [bankB[bb], KRB], [mB[OMT]])
            bb = nb()
            for j in range(4):
                pe(lambda e, bb=bb, j=j: e.matmul(banks[bb][:, j * 128:(j + 1) * 128], m_[LKT][:, j, :], m_[VTt][:, j, :], start=True, stop=False), [mB[LKT], mB[VTt]], [bankB[bb]])
                pe(lambda e, bb=bb, j=j: e.matmul(banks[bb][:, j * 128:(j + 1) * 128], m_[LBT][:, j, :], TKW[:, j, 128:256], start=False, stop=True), [mB[LBT], TKWB], [bankB[bb]])
            act(lambda e, bb=bb, Y0=Y0: e.activation(out=m_[Y0][:].rearrange("p j x -> p (j x)"), in_=banks[bb][:, :], func=AF.Identity), [bankB[bb]], [mB[Y0]])
            by = nb()
            for j in range(4):
                c = 4 * hb + j
                bh = nb()
                if bh == by:
                    bh = nb()
                pe(lambda e, bh=bh, j=j, c=c, PHI=PHI: e.matmul(banks[bh][:, 0:128], m_[PHI][:, j, :], Hs[:, c, :], start=True, stop=True), [mB[PHI], HsB[c]], [bankB[bh]])
                dve(lambda e, bh=bh, j=j, c=c, PSI=PSI: e.scalar_tensor_tensor(out=Hs[:, c + 1, :], in0=banks[bh][:, 0:128], scalar=t_[E1][:, c * CH + CH - 1:c * CH + CH], in1=m_[PSI][:, j, :], op0=ALU.mult, op1=ALU.add),
                    [bankB[bh], tB[E1], mB[PSI]], [HsB[c + 1]])
                pe(lambda e, by=by, j=j, c=c, OMT=OMT: e.matmul(banks[by][:, j * 128:(j + 1) * 128], m_[OMT][:, j, :], Hs[:, c, :], start=True, stop=True), [mB[OMT], HsB[c]], [bankB[by]])
            dve(lambda e, by=by, Y0=Y0, cs=cs: e.tensor_tensor(out=Ysb[:, cs, :], in0=banks[by][:, :].rearrange("p (j x) -> p j x", x=128), in1=m_[Y0][:], op=ALU.add), [bankB[by], mB[Y0]], [YsbB])
        dve(lambda e: e.tensor_copy(out=Hs[:, 0, :], in_=Hs[:, 8, :]), [HsB[8]], [HsB[0]])
        SQ = t_[TMP][:].rearrange("p (c x) -> p c x", x=64)
        YD = t_[E2][:].rearrange("p (c x) -> p c x", x=64)
        for h in range(2):
            ps = slice(64 * h, 64 * h + 64)
            dve(lambda e, ps=ps, h=h: e.tensor_copy(out=YD[ps, :, :], in_=Ysb[ps, :, 64 * h:64 * h + 64]), [YsbB], [tB[E2]])
        dve(lambda e: e.tensor_reduce(out=gs[:, :, 0], in_=YD, axis=AX.X, op=ALU.add), [tB[E2]], [gsB])
        act(lambda e: e.activation(out=SQ, in_=YD, func=AF.Square), [tB[E2]], [tB[TMP]])
        dve(lambda e: e.tensor_reduce(out=gs[:, :, 1], in_=SQ, axis=AX.X, op=ALU.add), [tB[TMP]], [gsB])
        dve(lambda e: e.tensor_scalar(out=gs[:, :, 0], in0=gs[:, :, 0], scalar1=1.0 / 64, scalar2=None, op0=ALU.mult), [gsB], [gsB])
        dve(lambda e: e.tensor_tensor(out=gs[:, :, 2], in0=gs[:, :, 0], in1=gs[:, :, 0], op=ALU.mult), [gsB], [gsB])
        dve(lambda e: e.scalar_tensor_tensor(out=gs[:, :, 1], in0=gs[:, :, 1], scalar=1.0 / 64, in1=gs[:, :, 2], op0=ALU.mult, op1=ALU.subtract), [gsB], [gsB])
        dve(lambda e: e.tensor_scalar(out=gs[:, :, 1], in0=gs[:, :, 1], scalar1=64e-5, scalar2=None, op0=ALU.add), [gsB], [gsB])
        act(lambda e: e.activation(out=gs[:, :, 1], in_=gs[:, :, 1], func=AF.Sqrt), [gsB], [gsB])
        dve(lambda e: e.reciprocal(out=gs[:, :, 1], in_=gs[:, :, 1]), [gsB], [gsB])
        dve(lambda e: e.tensor_tensor(out=YD, in0=YD, in1=gs[:, :, 0:1].to_broadcast([128, 8, 64]), op=ALU.subtract), [tB[E2], gsB], [tB[E2]])
        dve(lambda e: e.tensor_tensor(out=YD, in0=YD, in1=gs[:, :, 1:2].to_broadcast([128, 8, 64]), op=ALU.mult), [tB[E2], gsB], [tB[E2]])
        for h in range(2):
            ps = slice(64 * h, 64 * h + 64)
            dve(lambda e, ps=ps, h=h: e.tensor_copy(out=Ysb[ps, :, 64 * h:64 * h + 64], in_=YD[ps, :, :]), [tB[E2]], [YsbB])
        YF = E3
        for hb in range(2):
            bb = nb()
            for j in range(4):
                c = 4 * hb + j
                pe(lambda e, bb=bb, j=j, c=c: e.transpose(banks[bb][:, j * 128:(j + 1) * 128], Ysb[:, c, :], I_), [YsbB, cB], [bankB[bb]])
            for h in range(2):
                ps = slice(64 * h, 64 * h + 64)
                act(lambda e, bb=bb, ps=ps, h=h, hb=hb: e.activation(out=t_[YF][ps, hb * 256:(hb + 1) * 256].rearrange("p (j t) -> p j t", t=64),
                                                                    in_=banks[bb][ps, :].rearrange("p (j x) -> p j x", x=128)[:, :, 64 * h:64 * h + 64], func=AF.Identity,
                                                                    scale=rp[ps, 4:5], bias=rp[ps, 5:6]), [bankB[bb], cB], [tB[YF]])
        dve(lambda e: e.tensor_tensor(out=t_[YF][:], in0=t_[YF][:], in1=t_[BON][:], op=ALU.add), [tB[YF], tB[BON]], [tB[YF]])
        dve(lambda e: e.tensor_tensor(out=t_[YF][:], in0=t_[YF][:], in1=t_[GG][:], op=ALU.mult), [tB[YF], tB[GG]], [tB[YF]])
        p.dma("sp", lambda e, t0=t0: e.dma_start(out=T["yT"][0:128, t0:t0 + ST], in_=t_[YF][:]), reads=[tB[YF]], tag="ya")

        XC, RR, II, AAb, UU, HH = 19, 20, 6, 7, 8, 9
        dve(lambda e: e.tensor_scalar(out=t_[XC][:], in0=zxb[:, 3:ST + 3], scalar1=CW[3], scalar2=CBI, op0=ALU.mult, op1=ALU.add), [zxbB, cB], [tB[XC]])
        for j in range(3):
            dve(lambda e, j=j: e.scalar_tensor_tensor(out=t_[XC][:], in0=zxb[:, j:j + ST], scalar=CW[j], in1=t_[XC][:], op0=ALU.mult, op1=ALU.add), [zxbB, tB[XC], cB], [tB[XC]])
        act(lambda e: e.activation(out=zxb[:, 0:3], in_=zxb[:, ST:ST + 3], func=AF.Identity), [zxbB], [zxbB])
        b = nb()
        pe(lambda e, b=b: e.matmul(banks[b][:, :], wr[:], t_[XC][:], start=True, stop=True), [cB, tB[XC]], [bankB[b]])
        act(lambda e, b=b: e.activation(out=t_[RR][:], in_=banks[b][:, :], func=AF.Sigmoid, bias=BR), [bankB[b], cB], [tB[RR]])
        b = nb()
        pe(lambda e, b=b: e.matmul(banks[b][:, :], wi[:], t_[XC][:], start=True, stop=True), [cB, tB[XC]], [bankB[b]])
        act(lambda e, b=b: e.activation(out=t_[II][:], in_=banks[b][:, :], func=AF.Sigmoid, bias=BI), [bankB[b], cB], [tB[II]])
        act(lambda e: e.activation(out=t_[AAb][:], in_=t_[RR][:], func=AF.Exp, scale=SP8), [tB[RR], cB], [tB[AAb]])
        act(lambda e: e.activation(out=t_[UU][:], in_=t_[RR][:], func=AF.Exp, scale=SP16), [tB[RR], cB], [tB[UU]])
        dve(lambda e: e.tensor_scalar(out=t_[UU][:], in0=t_[UU][:], scalar1=-1.0, scalar2=1.0, op0=ALU.mult, op1=ALU.add), [tB[UU]], [tB[UU]])
        act(lambda e: e.activation(out=t_[UU][:], in_=t_[UU][:], func=AF.Sqrt), [tB[UU]], [tB[UU]])
        dve(lambda e: e.tensor_tensor(out=t_[II][:], in0=t_[II][:], in1=t_[XC][:], op=ALU.mult), [tB[II], tB[XC]], [tB[II]])
        dve(lambda e: e.tensor_tensor(out=t_[UU][:], in0=t_[UU][:], in1=t_[II][:], op=ALU.mult), [tB[UU], tB[II]], [tB[UU]])
        dve(lambda e: e.tensor_tensor_scan(out=t_[HH][:], data0=t_[AAb][:], data1=t_[UU][:], initial=hst[:, 0:1], op0=ALU.mult, op1=ALU.add), [tB[AAb], tB[UU], hstB], [tB[HH]])
        act(lambda e: e.activation(out=hst[:, 0:1], in_=t_[HH][:, ST - 1:ST], func=AF.Identity), [tB[HH]], [hstB])
        dve(lambda e: e.tensor_tensor(out=t_[HH][:], in0=t_[HH][:], in1=t_[21][:], op=ALU.mult), [tB[HH], tB[21]], [tB[HH]])
        p.dma("sp", lambda e, t0=t0: e.dma_start(out=T["yT"][128:256, t0:t0 + ST], in_=t_[HH][:]), reads=[tB[HH]], tag="yb")


D = 2048
NKP = 71 * 128
NEG = -1.0e30
DEBUG = False


def build_odd1(nc):
    def din(name, shape):
        return nc.dram_tensor(name, list(shape), F32, kind="ExternalInput").ap()
    T = dict(xT=din("xT", [D, 1024]), w=din("w", [D, 320]), nv=din("nv", [128, 4]))
    T["ckvT"] = nc.dram_tensor("ckvT", [256, 1024], F32, kind="ExternalOutput").ap()
    T["kidxT"] = nc.dram_tensor("kidxT", [64, 1024], F32, kind="ExternalOutput").ap()
    p = Prog(nc)
    emit_odd1(p, nc, T)
    print("odd1 stats", p.finish())
    return nc


def emit_odd1(p, nc, T):
    xt = p.sbuf("xt", [128, 16, 512], BF16); xtB = p.buf()
    w = p.sbuf("w", [128, 16, 320], BF16); wB = p.buf()
    nv = p.sbuf("nv", [128, 4], F32); cB = p.buf()
    ones = p.sbuf("ones", [128, 128], F32); eps = p.sbuf("eps", [128, 2], F32)
    t_ = [p.sbuf(f"t{i}", [128, 512], F32) for i in range(8)]; tB = p.bufs(8)
    banks = [p.psum(f"bank{i}", [128, 512]) for i in range(8)]; bankB = p.bufs(8)
    dve = lambda fn, R, W: p.add("dve", fn, reads=R, writes=W)
    act = lambda fn, R, W: p.add("act", fn, reads=R, writes=W)
    pe = lambda fn, R, W: p.add("pe", fn, reads=R, writes=W)
    p.dma("sp", lambda e: e.dma_start(out=nv[:], in_=T["nv"][:]), writes=[cB], tag="c")
    p.dma("pool", lambda e: e.dma_start(out=w[:], in_=T["w"].rearrange("(k p) n -> p k n", p=128)), writes=[wB], tag="w")
    dve(lambda e: e.memset(ones[:], 1.0), [], [cB])
    dve(lambda e: e.memset(eps[:, 0:1], 1e-6), [], [cB])
    dve(lambda e: e.memset(eps[:, 1:2], 1e-5), [], [cB])
    xv = T["xT"].rearrange("(k p) n -> p k n", p=128)
    for g in range(2):
        cs = slice(g * 512, g * 512 + 512)
        p.dma("pool", lambda e, cs=cs: e.dma_start(out=xt[:], in_=xv[:, :, cs]), writes=[xtB], tag="xt")
        for ci, (c0, cn) in enumerate([(0, 128), (128, 128), (256, 64)]):
            b = ci
            for k in range(16):
                pe(lambda e, b=b, k=k, c0=c0, cn=cn: e.matmul(banks[b][0:cn, :], w[:, k, c0:c0 + cn], xt[:, k, :], start=(k == 0), stop=(k == 15)), [wB, xtB], [bankB[b]])
            act(lambda e, b=b, ci=ci, cn=cn: e.activation(out=t_[ci][0:cn, :], in_=banks[b][0:cn, :], func=AF.Identity), [bankB[b]], [tB[ci]])
            act(lambda e, ci=ci, cn=cn: e.activation(out=t_[3 + ci][0:cn, :], in_=t_[ci][0:cn, :], func=AF.Square), [tB[ci]], [tB[3 + ci]])
        pe(lambda e: e.matmul(banks[3][:, :], ones[:], t_[3][:], start=True, stop=False), [cB, tB[3]], [bankB[3]])
        pe(lambda e: e.matmul(banks[3][:, :], ones[:], t_[4][:], start=False, stop=True), [cB, tB[4]], [bankB[3]])
        act(lambda e: e.activation(out=t_[6][:], in_=banks[3][:, :], func=AF.Sqrt, scale=1.0 / 256, bias=eps[:, 0:1]), [bankB[3], cB], [tB[6]])
        dve(lambda e: e.reciprocal(out=t_[6][:], in_=t_[6][:]), [tB[6]], [tB[6]])
        for ci in range(2):
            dve(lambda e, ci=ci: e.tensor_tensor(out=t_[ci][:], in0=t_[ci][:], in1=t_[6][:], op=ALU.mult), [tB[ci], tB[6]], [tB[ci]])
            act(lambda e, ci=ci: e.activation(out=t_[ci][:], in_=t_[ci][:], func=AF.Identity, scale=nv[:, ci:ci + 1]), [tB[ci], cB], [tB[ci]])
            p.dma("sp", lambda e, ci=ci, cs=cs: e.dma_start(out=T["ckvT"][ci * 128:(ci + 1) * 128, cs], in_=t_[ci][:]), reads=[tB[ci]], tag=f"o{ci}")
        pe(lambda e: e.matmul(banks[4][0:64, :], ones[0:64, 0:64], t_[2][0:64, :], start=True, stop=True), [cB, tB[2]], [bankB[4]])
        pe(lambda e: e.matmul(banks[5][0:64, :], ones[0:64, 0:64], t_[5][0:64, :], start=True, stop=True), [cB, tB[5]], [bankB[5]])
        act(lambda e: e.activation(out=t_[6][0:64, :], in_=banks[4][0:64, :], func=AF.Identity, scale=1.0 / 64), [bankB[4]], [tB[6]])
        dve(lambda e: e.tensor_tensor(out=t_[7][0:64, :], in0=t_[6][0:64, :], in1=t_[6][0:64, :], op=ALU.mult), [tB[6]], [tB[7]])
        dve(lambda e: e.scalar_tensor_tensor(out=t_[7][0:64, :], in0=banks[5][0:64, :], scalar=1.0 / 64, in1=t_[7][0:64, :], op0=ALU.mult, op1=ALU.subtract), [bankB[5], tB[7]], [tB[7]])
        act(lambda e: e.activation(out=t_[7][0:64, :], in_=t_[7][0:64, :], func=AF.Sqrt, bias=eps[0:64, 1:2]), [tB[7], cB], [tB[7]])
        dve(lambda e: e.reciprocal(out=t_[7][0:64, :], in_=t_[7][0:64, :]), [tB[7]], [tB[7]])
        dve(lambda e: e.tensor_tensor(out=t_[2][0:64, :], in0=t_[2][0:64, :], in1=t_[6][0:64, :], op=ALU.subtract), [tB[2], tB[6]], [tB[2]])
        dve(lambda e: e.tensor_tensor(out=t_[2][0:64, :], in0=t_[2][0:64, :], in1=t_[7][0:64, :], op=ALU.mult), [tB[2], tB[7]], [tB[2]])
        act(lambda e: e.activation(out=t_[2][0:64, :], in_=t_[2][0:64, :], func=AF.Identity, scale=nv[0:64, 2:3], bias=nv[0:64, 3:4]), [tB[2], cB], [tB[2]])
        p.dma("sp", lambda e, cs=cs: e.dma_start(out=T["kidxT"][:, cs], in_=t_[2][0:64, :]), reads=[tB[2]], tag="o2")


def build_odd2(nc, nqb=8):
    def din(name, shape):
        return nc.dram_tensor(name, list(shape), F32, kind="ExternalInput").ap()
    T = dict(xT=din("xT", [D, 1024]), ckvT=din("ckvT", [256, NKP]), kidxT=din("kidxT", [64, NKP]), ckvM=din("ckvM", [NKP, 256]),
             km0=din("km0", [128, 1024]), dmask=din("dmask", [128, 128]),
             w_q=din("w_q", [D, 528]), qn=din("qn", [128, 4]), w_uq=din("w_uq", [512, 2048]), w_ukT=din("w_ukT", [128, 16, 256]),
             w_uv=din("w_uv", [16, 256, 128]), w_qidx=din("w_qidx", [512, 1024]), nearb=din("nearb", [2, 128, 2048]), farb=din("farb", [128, 16]),
             ident=din("ident", [128, 128]))
    T["attnT"] = nc.dram_tensor("attnT", [D, 1024], F32, kind="ExternalOutput").ap()
    if DEBUG:
        T["dbg_sc"] = nc.dram_tensor("dbg_sc", [128, 8192], F32, kind="ExternalOutput").ap()
        T["dbg_mb"] = nc.dram_tensor("dbg_mb", [128, 8192], F32, kind="ExternalOutput").ap()
        T["dbg_wq"] = nc.dram_tensor("dbg_wq", [128, 16], F32, kind="ExternalOutput").ap()
    T["ckvb"] = nc.dram_tensor("ckvb", [256, NKP], BF16).ap()
    T["ckvmb"] = nc.dram_tensor("ckvmb", [NKP, 256], BF16).ap()
    T["biasp"] = nc.dram_tensor("biasp", [2, 128, 2048], F32).ap()
    p = Prog(nc)
    emit_odd2(p, nc, T, nqb)
    print("odd2 stats", p.finish())
    return nc


def emit_odd2(p, nc, T, nqb=8):
    dve = lambda fn, R, W: p.add("dve", fn, reads=R, writes=W)
    act = lambda fn, R, W: p.add("act", fn, reads=R, writes=W)
    pe = lambda fn, R, W: p.add("pe", fn, reads=R, writes=W)
    sc = p.sbuf("sc", [128, 8192], F32); scB = p.buf("sc")
    xtg = sc[:].bitcast(BF16)[:, 0:8192].rearrange("p (k n) -> p k n", k=16)
    mb = p.sbuf("mb", [128, 8448], BF16); mbB = p.buf("mb")
    w_q = mb[:, 0:8448].rearrange("p (k n) -> p k n", k=16)
    kidx = p.sbuf("kidx", [128, NKP], BF16); kidxB = p.buf()
    w_uq = p.sbuf("w_uq", [128, 4, 2048], BF16); w_qidx = p.sbuf("w_qidx", [128, 4, 1024], BF16)
    w_uk = p.sbuf("w_uk", [128, 16, 256], BF16); w_uv = p.sbuf("w_uv", [128, 16, 2, 128], BF16); wB = p.buf("weights")
    cqn = p.sbuf("cqn", [128, 4, 1024], BF16); cqnB = p.buf()
    widxT = p.sbuf("widxT", [16, 1024], F32); widxB = p.buf()
    qn = p.sbuf("qn", [128, 4], F32); ident = p.sbuf("ident", [128, 128], F32); identb = p.sbuf("identb", [128, 512], BF16)
    onesf = p.sbuf("onesf", [128, 128], F32); onesb = p.sbuf("onesb", [128, 128], BF16); eps = p.sbuf("eps", [128, 1], F32)
    km0 = p.sbuf("km0", [128, 1024], F32); dmask = p.sbuf("dmask", [128, 128], F32); farb = p.sbuf("farb", [128, 16], F32)
    cB = p.buf("consts")
    NT = 8
    t_ = [p.sbuf(f"t{i}", [128, 512], F32) for i in range(NT)]; tB = p.bufs(NT, "t")
    big = p.sbuf("big", [128, 2048], F32); bigB = p.buf()
    qT = p.sbuf("qT", [128, 16, 128], BF16); qTB = p.buf()
    qabs = p.sbuf("qabs", [128, 2, 16, 128], BF16); qabsB = p.buf()
    qidx = p.sbuf("qidx", [128, 8, 128], BF16); qidxB = p.buf()
    wq = p.sbuf("wq", [128, 16], F32); wqB = p.buf()
    m8 = p.sbuf("m8", [128, 8], F32); m8B = p.buf()
    kTr = [p.sbuf(f"kTr{i}", [128, 2, 512], BF16) for i in range(3)]; kTrB = p.bufs(3)
    kMr = [p.sbuf(f"kMr{i}", [128, 4, 256], BF16) for i in range(3)]; kMrB = p.bufs(3)
    pTt = [p.sbuf(f"pT{i}", [128, 512], BF16) for i in range(3)]; pTB = p.bufs(3)
    bsl = [p.sbuf(f"bsl{i}", [128, 2, 512], F32) for i in range(2)]; bslB = p.bufs(2)
    on = p.sbuf("on", [128, 2, 512], BF16); onB = p.buf()
    ost = p.sbuf("ost", [128, 4, 128], F32); ostB = p.buf()
    banks = [p.psum(f"bank{i}", [128, 512]) for i in range(8)]; bankB = p.bufs(8, "bank")
    rr = [0]

    def nb(lo=0, hi=8):
        rr[0] += 1
        return lo + rr[0] % (hi - lo)

    for dst, src in [(qn, "qn"), (ident, "ident"), (km0, "km0"), (dmask, "dmask"), (farb, "farb")]:
        p.dma("sp", lambda e, dst=dst, src=src: e.dma_start(out=dst[:], in_=T[src][:]), writes=[cB], tag="c")
    dve(lambda e: e.memset(onesf[:], 1.0), [], [cB])
    dve(lambda e: e.memset(onesb[:], 1.0), [], [cB])
    dve(lambda e: e.memset(eps[:], 1e-6), [], [cB])
    for j in range(4):
        dve(lambda e, j=j: e.tensor_copy(out=identb[:, j * 128:(j + 1) * 128], in_=ident[:]), [cB], [cB])
    p.dma("pool", lambda e: e.dma_start(out=w_q, in_=T["w_q"].rearrange("(k p) n -> p k n", p=128)), writes=[mbB], tag="w")
    p.dma("pool", lambda e: e.dma_start(out=w_uq[:], in_=T["w_uq"].rearrange("(k p) n -> p k n", p=128)), writes=[wB], tag="w")
    p.dma("pool", lambda e: e.dma_start(out=w_qidx[:], in_=T["w_qidx"].rearrange("(k p) n -> p k n", p=128)), writes=[wB], tag="w")
    p.dma("pool", lambda e: e.dma_start(out=w_uk[:], in_=T["w_ukT"][:]), writes=[wB], tag="w")
    p.dma("pool", lambda e: e.dma_start(out=w_uv[:], in_=T["w_uv"].rearrange("h (a p) d -> p h a d", p=128)), writes=[wB], tag="w")
    for hh in range(2):
        p.dma("pool", lambda e, hh=hh: e.dma_start(out=kidx[64 * hh:64 * hh + 64, :], in_=T["kidxT"][:, :]), writes=[kidxB], tag="w")
    ckvbB = p.buf("ckvb"); ckvmbB = p.buf("ckvmb"); biaspB = p.buf("biasp")
    p.dma("pool", lambda e: e.dma_start(out=T["ckvb"][:, :], in_=T["ckvT"][:, :]), writes=[ckvbB], tag="kc")
    p.dma("pool", lambda e: e.dma_start(out=T["ckvmb"][:, :], in_=T["ckvM"][:, :]), writes=[ckvmbB], tag="kc")
    for o in range(2):
        p.dma("sp", lambda e, o=o: e.dma_start(out=big[:], in_=T["nearb"][o]), writes=[bigB], tag="nb")
        dve(lambda e: e.tensor_tensor(out=big[:].rearrange("p (h q) -> p h q", h=16), in0=big[:].rearrange("p (h q) -> p h q", h=16),
                                      in1=farb[:].unsqueeze(2).to_broadcast([128, 16, 128]), op=ALU.subtract), [bigB, cB], [bigB])
        p.dma("sp", lambda e, o=o: e.dma_start(out=T["biasp"][o], in_=big[:]), reads=[bigB], writes=[biaspB], tag="nbs")

    xv = T["xT"].rearrange("(k p) n -> p k n", p=128)
    for g in range(2):
        cs = slice(g * 512, g * 512 + 512)
        p.dma("pool", lambda e, cs=cs: e.dma_start(out=xtg, in_=xv[:, :, cs]), writes=[scB], tag="xt")
        for ct in range(4):
            b = nb()
            for k in range(16):
                pe(lambda e, b=b, k=k, ct=ct: e.matmul(banks[b][:, :], w_q[:, k, ct * 128:(ct + 1) * 128], xtg[:, k, :], start=(k == 0), stop=(k == 15)), [mbB, scB], [bankB[b]])
            act(lambda e, b=b, ct=ct: e.activation(out=t_[ct][:], in_=banks[b][:, :], func=AF.Identity), [bankB[b]], [tB[ct]])
            act(lambda e, ct=ct: e.activation(out=t_[4 + ct][:], in_=t_[ct][:], func=AF.Square), [tB[ct]], [tB[4 + ct]])
        b = nb()
        for k in range(16):
            pe(lambda e, b=b, k=k: e.matmul(banks[b][0:16, :], w_q[:, k, 512:528], xtg[:, k, :], start=(k == 0), stop=(k == 15)), [mbB, scB], [bankB[b]])
        act(lambda e, b=b, cs=cs: e.activation(out=widxT[:, cs], in_=banks[b][0:16, :], func=AF.Identity, scale=1.0 / 32), [bankB[b]], [widxB])
        b = nb()
        for ct in range(4):
            pe(lambda e, b=b, ct=ct: e.matmul(banks[b][:, :], onesf[:], t_[4 + ct][:], start=(ct == 0), stop=(ct == 3)), [cB, tB[4 + ct]], [bankB[b]])
        act(lambda e, b=b: e.activation(out=t_[4][:], in_=banks[b][:, :], func=AF.Sqrt, scale=1.0 / 512, bias=eps[:, 0:1]), [bankB[b], cB], [tB[4]])
        dve(lambda e: e.reciprocal(out=t_[4][:], in_=t_[4][:]), [tB[4]], [tB[4]])
        for ct in range(4):
            dve(lambda e, ct=ct: e.tensor_tensor(out=t_[ct][:], in0=t_[ct][:], in1=t_[4][:], op=ALU.mult), [tB[ct], tB[4]], [tB[ct]])
            act(lambda e, ct=ct, cs=cs: e.activation(out=cqn[:, ct, cs], in_=t_[ct][:], func=AF.Identity, scale=qn[:, ct:ct + 1]), [tB[ct], cB], [cqnB])

    ring_i = [0]
    kT_v = T["ckvb"].rearrange("(a p) n -> p a n", p=128)
    kM_v = T["ckvmb"].rearrange("(t p) r -> p t r", p=128)

    for i in range(nqb):
        qs = slice(i * 128, i * 128 + 128)
        nk = 1024 * (i + 1)
        nt128 = 8 * (i + 1)
        for half in range(2):
            b = nb()
            for m in range(4):
                mm = 4 * half + m
                for k in range(4):
                    pe(lambda e, b=b, m=m, mm=mm, k=k, qs=qs: e.matmul(banks[b][:, m * 128:(m + 1) * 128], w_qidx[:, k, mm * 128:(mm + 1) * 128], cqn[:, k, qs], start=(k == 0), stop=(k == 3)), [wB, cqnB], [bankB[b]])
            act(lambda e, b=b, half=half: e.activation(out=qidx[:, 4 * half:4 * half + 4, :].rearrange("p a b -> p (a b)"), in_=banks[b][:, :], func=AF.Identity), [bankB[b]], [qidxB])
        for hq in range(4):
            b = nb()
            for m in range(4):
                h = 4 * hq + m
                for k in range(4):
                    pe(lambda e, b=b, m=m, h=h, k=k, qs=qs: e.matmul(banks[b][:, m * 128:(m + 1) * 128], w_uq[:, k, h * 128:(h + 1) * 128], cqn[:, k, qs], start=(k == 0), stop=(k == 3)), [wB, cqnB], [bankB[b]])
            act(lambda e, b=b, hq=hq: e.activation(out=qT[:, 4 * hq:4 * hq + 4, :].rearrange("p a b -> p (a b)"), in_=banks[b][:, :], func=AF.Identity), [bankB[b]], [qTB])
        for a in range(2):
            for hq in range(4):
                b = nb()
                for m in range(4):
                    h = 4 * hq + m
                    pe(lambda e, b=b, m=m, h=h, a=a: e.matmul(banks[b][:, m * 128:(m + 1) * 128], w_uk[:, h, a * 128:(a + 1) * 128], qT[:, h, :], start=True, stop=True), [wB, qTB], [bankB[b]])
                act(lambda e, b=b, hq=hq, a=a: e.activation(out=qabs[:, a, 4 * hq:4 * hq + 4, :].rearrange("p a b -> p (a b)"), in_=banks[b][:, :], func=AF.Identity, scale=128 ** -0.5), [bankB[b]], [qabsB])
        b = nb()
        pe(lambda e, b=b, qs=qs: e.transpose(banks[b][:, 0:16], widxT[0:16, qs], ident[0:16, 0:16]), [widxB, cB], [bankB[b]])
        act(lambda e, b=b: e.activation(out=wq[:], in_=banks[b][:, 0:16], func=AF.Identity), [bankB[b]], [wqB])

        for kt in range(nk // 512):
            ks = slice(kt * 512, kt * 512 + 512)
            for j in range(16):
                b = nb()
                hp = slice(64 * (j % 2), 64 * (j % 2) + 64)
                pe(lambda e, b=b, j=j, hp=hp, ks=ks: e.matmul(banks[b][:, :], qidx[hp, j // 2, :], kidx[hp, ks], start=True, stop=True), [qidxB, kidxB], [bankB[b]])
                r = 4 + j % 4
                act(lambda e, b=b, r=r: e.activation(out=t_[r][:], in_=banks[b][:, :], func=AF.Relu), [bankB[b]], [tB[r]])
                if j == 0 and kt >= 2:
                    dve(lambda e, r=r, ks=ks: e.tensor_scalar(out=sc[:, ks], in0=t_[r][:], scalar1=wq[:, 0:1], scalar2=None, op0=ALU.mult), [tB[r], wqB], [scB])
                else:
                    in1 = (lambda ks=ks: km0[:, ks]) if j == 0 else (lambda ks=ks: sc[:, ks])
                    dve(lambda e, r=r, ks=ks, j=j, in1=in1: e.scalar_tensor_tensor(out=sc[:, ks], in0=t_[r][:], scalar=wq[:, j:j + 1], in1=in1(), op0=ALU.mult, op1=ALU.add), [tB[r], wqB, scB, cB], [scB])
        if True:
            dve(lambda e, nk=nk: e.tensor_tensor(out=sc[:, nk - 128:nk], in0=sc[:, nk - 128:nk], in1=dmask[:], op=ALU.add), [scB, cB], [scB])
        if DEBUG and i == nqb - 1:
            p.dma("sp", lambda e, nk=nk: e.dma_start(out=T["dbg_sc"][:, 0:nk], in_=sc[:, 0:nk]), reads=[scB], tag="dbg")
            p.dma("sp", lambda e: e.dma_start(out=T["dbg_wq"][:, :], in_=wq[:]), reads=[wqB], tag="dbg")
        for rnd in range(32):
            dve(lambda e, nk=nk: e.max(out=m8[:], in_=sc[:, 0:nk]), [scB], [m8B])
            dve(lambda e, nk=nk: e.match_replace(out=sc[:, 0:nk], in_to_replace=m8[:], in_values=sc[:, 0:nk], imm_value=-3.0e38), [scB, m8B], [scB])
        dve(lambda e, nk=nk: e.tensor_scalar(out=mb[:, 0:nk], in0=sc[:, 0:nk], scalar1=-2.0e38, scalar2=-30000.0, op0=ALU.is_gt, op1=ALU.mult), [scB], [mbB])
        if i == 0:
            dve(lambda e: e.tensor_tensor(out=mb[:, 0:1024], in0=mb[:, 0:1024], in1=km0[:], op=ALU.add), [mbB, cB], [mbB])
            dve(lambda e: e.tensor_tensor(out=mb[:, 896:1024], in0=mb[:, 896:1024], in1=dmask[:], op=ALU.add), [mbB, cB], [mbB])

        if DEBUG and i == nqb - 1:
            dve(lambda e, nk=nk: e.tensor_copy(out=sc[:, 0:nk], in_=mb[:, 0:nk]), [mbB], [scB])
            p.dma("sp", lambda e, nk=nk: e.dma_start(out=T["dbg_mb"][:, 0:nk], in_=sc[:, 0:nk]), reads=[scB], tag="dbg")
        for hg in range(4):
            bs = hg % 2
            p.dma("sp", lambda e, hg=hg, bs=bs: e.dma_start(out=bsl[bs][:], in_=T["biasp"][:, :, hg * 512:(hg + 1) * 512].rearrange("o p x -> p o x")), reads=[biaspB], writes=[bslB[bs]], tag=f"bsl{bs}")
            for g4 in range(nt128 // 4):
                s = ring_i[0] % 3
                ring_i[0] += 1
                p.dma("sp", lambda e, s=s, g4=g4: e.dma_start(out=kTr[s][:], in_=kT_v[:, :, g4 * 512:(g4 + 1) * 512]), reads=[ckvbB], writes=[kTrB[s]], tag=f"kT{s}")
                p.dma("sp", lambda e, s=s, g4=g4: e.dma_start(out=kMr[s][:], in_=kM_v[:, 4 * g4:4 * g4 + 4, :]), reads=[ckvmbB], writes=[kMrB[s]], tag=f"kM{s}")
                for u in range(4):
                    kt = 4 * g4 + u
                    lg = kt % 2
                    first, last = (kt == 0), (kt == nt128 - 1)
                    for a in range(2):
                        pe(lambda e, lg=lg, s=s, u=u, a=a, hg=hg: e.matmul(banks[lg][:, :], kTr[s][:, a, u * 128:(u + 1) * 128], qabs[:, a, 4 * hg:4 * hg + 4, :].rearrange("p a b -> p (a b)"), start=(a == 0), stop=False),
                           [kTrB[s], qabsB], [bankB[lg]])
                    pe(lambda e, lg=lg, kt=kt: e.matmul(banks[lg][:, :], mb[:, kt * 128:(kt + 1) * 128], identb[:], start=False, stop=True), [mbB, cB], [bankB[lg]])
                    pt = kt % 3
                    near = kt - (nt128 - 2)
                    if near >= 0:
                        r = 4 + kt % 2
                        dve(lambda e, lg=lg, r=r, bs=bs, near=near: e.tensor_tensor(out=t_[r][:], in0=banks[lg][:, :], in1=bsl[bs][:, near, :], op=ALU.add), [bankB[lg], bslB[bs]], [tB[r]])
                        act(lambda e, r=r, pt=pt: e.activation(out=pTt[pt][:], in_=t_[r][:], func=AF.Exp), [tB[r]], [pTB[pt]])
                    else:
                        act(lambda e, lg=lg, pt=pt: e.activation(out=pTt[pt][:], in_=banks[lg][:, :], func=AF.Exp), [bankB[lg]], [pTB[pt]])
                    for a in range(2):
                        pe(lambda e, s=s, u=u, a=a, pt=pt, first=first, last=last: e.matmul(banks[2 + a][:, :], kMr[s][:, u, a * 128:(a + 1) * 128], pTt[pt][:], start=first, stop=last), [kMrB[s], pTB[pt]], [bankB[2 + a]])
                    pe(lambda e, pt=pt, first=first, last=last: e.matmul(banks[4][:, :], onesb[:], pTt[pt][:], start=first, stop=last), [cB, pTB[pt]], [bankB[4]])
            dve(lambda e: e.reciprocal(out=t_[0][:], in_=banks[4][:, :]), [bankB[4]], [tB[0]])
            for a in range(2):
                dve(lambda e, a=a: e.tensor_tensor(out=on[:, a, :], in0=banks[2 + a][:, :], in1=t_[0][:], op=ALU.mult), [bankB[2 + a], tB[0]], [onB])
            b = nb(5, 8)
            for m in range(4):
                h = 4 * hg + m
                for a in range(2):
                    pe(lambda e, b=b, m=m, h=h, a=a: e.matmul(banks[b][:, m * 128:(m + 1) * 128], w_uv[:, h, a, :], on[:, a, m * 128:(m + 1) * 128], start=(a == 0), stop=(a == 1)), [wB, onB], [bankB[b]])
            act(lambda e, b=b: e.activation(out=ost[:].rearrange("p a b -> p (a b)"), in_=banks[b][:, :], func=AF.Identity), [bankB[b]], [ostB])
            p.dma("sp", lambda e, hg=hg, qs=qs: e.dma_start(out=T["attnT"][hg * 512:(hg + 1) * 512, qs].rearrange("(m p) q -> p m q", p=128), in_=ost[:]), reads=[ostB], tag="out")

_PROGS = {}
N_CORES = 8
SEQ = 8192


def _prog(name):
    if name not in _PROGS:
        nc = bass.Bass("TRN2", target_bir_lowering=False)
        {"even": build_even, "tail": build_tail, "odd1": build_odd1, "odd2": build_odd2}[name](nc)
        _PROGS[name] = nc
    return _PROGS[name]


def _run(name, in_maps):
    in_maps = [{k: np.ascontiguousarray(v, dtype=np.float32) for k, v in m.items()} for m in in_maps]
    res = run_bass_kernel_spmd(_prog(name), in_maps, core_ids=list(range(N_CORES)))
    return res.results


def _pm(v, t):
    return np.ascontiguousarray(np.asarray(v).reshape(t, 128).T)


_CONST = {}


def _consts():
    if _CONST:
        return _CONST
    blk = np.kron(np.eye(2), np.ones((64, 64))).astype(np.float32)
    r_, c_ = np.meshgrid(np.arange(128), np.arange(128), indexing="ij")
    _CONST["cm"] = np.ascontiguousarray(np.stack([np.eye(128), blk, -blk * (r_ < c_), blk * (r_ <= c_), -blk * (r_ > c_), blk * (r_ < c_)], 1).astype(np.float32))
    chm = np.ones((128, ST), np.float32); chm[:, ::CH] = 0
    _CONST["chm"] = chm
    dmask = np.zeros((128, 128), np.float32); dmask[:64, 64:] = NEG
    _CONST["dmask"] = dmask
    _CONST["ident"] = np.eye(128, dtype=np.float32)
    import math
    import jax
    import jax.numpy as jnp
    rel = np.arange(-255, 128)
    with jax.default_device(jax.devices("cpu")[0]):
        relj = jnp.asarray(rel)
        nbk = 16; max_exact = 8
        ret = jnp.where(relj > 0, nbk, 0)
        n = jnp.abs(relj)
        nf = jnp.maximum(n, 1).astype(jnp.float32)
        large = max_exact + (jnp.log(nf / max_exact) / math.log(128 / max_exact) * (nbk - max_exact)).astype(jnp.int32)
        large = jnp.minimum(large, nbk - 1)
        bucket = np.asarray(ret + jnp.where(n < max_exact, n, large))
    s_, q_ = np.meshgrid(np.arange(128), np.arange(128), indexing="ij")
    _CONST["bk"] = [bucket[(s_ - q_ - 128 * (1 - o)) + 255] for o in range(2)]
    return _CONST


def _even_inputs(I, j, xT, c):
    C = _consts()
    ch = slice(128 * c, 128 * c + 128)
    W = I["ev_w_in"][j]
    ar = np.arange(128 * c, 128 * c + 128)
    cols = np.concatenate([ar, 1024 + ar, 2048 + ar, np.arange(3072, 3200), np.arange(3200, 3360), 3360 + ar, 4384 + ar])
    mu_full = np.concatenate([I["a_mu"][j], np.zeros(2048, np.float32)])[cols]
    mu = np.zeros((128, 8), np.float32)
    for ci, (c0, cn) in enumerate(COLT):
        mu[:cn, ci] = mu_full[c0:c0 + cn]
    rp = np.zeros((128, 8), np.float32)
    for i, k in enumerate(["a_w0", "a_a0", "a_k_k", "a_k_a", "a_gn_g", "a_gn_b"]):
        rp[:, i] = I[k][j][ch]
    rp[:, 6] = I["a_r_k"][j].reshape(-1)[ch]
    lw = np.concatenate([I["a_w2"][j][:, ch], I["a_a2"][j][:, ch]], 0)
    bp = np.zeros((128, 8), np.float32)
    for t in range(4):
        bp[:, t] = I["b_conv_w"][j][t, ch]
    for i, k in enumerate(["b_conv_b", "b_b_r", "b_b_i", "b_lambda"]):
        bp[:, 4 + i] = I[k][j][ch]
    wr = np.zeros((128, 128), np.float32); wi = np.zeros((128, 128), np.float32)
    for b in range(2):
        wr[64 * b:64 * b + 64, 64 * b:64 * b + 64] = I["b_w_r"][j][2 * c + b]
        wi[64 * b:64 * b + 64, 64 * b:64 * b + 64] = I["b_w_i"][j][2 * c + b]
    return dict(xT=xT, w_in=W[:, cols], mu=mu, rp=rp, lw=lw, g2a=I["a_g2"][j][:128, ch], g2b=I["a_g2"][j][128:, ch],
                bp=bp, wr=wr, wi=wi, cm=C["cm"], chm=C["chm"])


def _tail_inputs(I, layer, w_out, mT, xT, c):
    own = slice(1024 * c, 1024 * c + 1024)
    def halo(a):
        h = a[:, 1024 * c - 2:1024 * c] if c > 0 else np.zeros((a.shape[0], 2), np.float32)
        return np.concatenate([h, a[:, own]], axis=1)
    cw = I["ffn_conv_w"][layer]
    return dict(mT=halo(mT), xT=halo(xT), pT=I["p"][layer, 0, own].T, flag=np.full((128, 1), 0.0 if c == 0 else 1.0, np.float32),
                w_out=w_out, w_up=I["ffn_w_up"][layer], w_down=I["ffn_w_down"][layer], w_pp=I["ple_w_proj"][layer], w_pg=I["ple_w_gate"][layer],
                lnv=np.stack([_pm(I["ln1_g"][layer], 16), _pm(I["ln1_b"][layer], 16), _pm(I["ln2_g"][layer], 16), _pm(I["ln2_b"][layer], 16)], axis=1),
                cwv=np.stack([_pm(cw[t], 88) for t in range(3)], axis=2), cbv=_pm(I["ffn_conv_b"][layer], 88))


def _odd1_inputs(I, j, xT, c):
    own = slice(1024 * c, 1024 * c + 1024)
    nv = np.zeros((128, 4), np.float32)
    nv[:, 0:2] = _pm(I["c_kv_norm"][j], 2); nv[:64, 2] = I["c_kidx_g"][j]; nv[:64, 3] = I["c_kidx_b"][j]
    return dict(xT=xT[:, own], w=I["od_w_in"][j][:, 512:832], nv=nv)


def _odd2_tokens(c):
    return np.concatenate([np.arange(128 * (8 * i + c), 128 * (8 * i + c) + 128) for i in range(8)])


def _odd2_inputs(I, j, xT, ckvT, kidxT, c):
    C = _consts()
    tok = _odd2_tokens(c)
    sh = (7 - c) * 128
    ckv_s = np.zeros((256, NKP), np.float32); ckv_s[:, sh:sh + SEQ] = ckvT
    kidx_s = np.zeros((64, NKP), np.float32); kidx_s[:, sh:sh + SEQ] = kidxT
    km0 = np.zeros((128, 1024), np.float32); km0[:, :sh] = NEG
    rb = I["rel_bias"]
    nearb = np.stack([rb[C["bk"][o]].transpose(0, 2, 1) for o in range(2)], 0).reshape(2, 128, 2048)
    W = I["od_w_in"][j]
    return dict(xT=xT[:, tok], ckvT=ckv_s, kidxT=kidx_s, ckvM=ckv_s.T, km0=km0, dmask=C["dmask"],
                w_q=np.concatenate([W[:, :512], W[:, 832:848]], 1), qn=_pm(I["c_q_norm"][j], 4), w_uq=I["c_w_uq"][j],
                w_ukT=I["c_w_uk"][j].transpose(2, 0, 1), w_uv=I["c_w_uv"][j], w_qidx=I["c_w_qidx"][j],
                nearb=nearb, farb=np.tile(rb[15][None, :], (128, 1)), ident=C["ident"])


def kernel(**inputs):
    I = {k: np.asarray(v, dtype=np.float32) for k, v in inputs.items()}
    xT = np.ascontiguousarray(I["x"][0].T)
    for layer in range(4):
        j = layer // 2
        mT = np.zeros((D, SEQ), np.float32)
        if layer % 2 == 0:
            res = _run("even", [_even_inputs(I, j, xT, c) for c in range(N_CORES)])
            for c in range(N_CORES):
                mT[128 * c:128 * c + 128] = res[c]["yT"][:128]
                mT[1024 + 128 * c:1024 + 128 * c + 128] = res[c]["yT"][128:]
            w_out = I["ev_w_out"][j]
        else:
            res = _run("odd1", [_odd1_inputs(I, j, xT, c) for c in range(N_CORES)])
            ckvT = np.concatenate([res[c]["ckvT"] for c in range(N_CORES)], axis=1)
            kidxT = np.concatenate([res[c]["kidxT"] for c in range(N_CORES)], axis=1)
            res = _run("odd2", [_odd2_inputs(I, j, xT, ckvT, kidxT, c) for c in range(N_CORES)])
            for c in range(N_CORES):
                mT[:, _odd2_tokens(c)] = res[c]["attnT"]
            w_out = I["od_w_out"][j]
        res = _run("tail", [_tail_inputs(I, layer, w_out, mT, xT, c) for c in range(N_CORES)])
        xT = np.concatenate([res[c]["xoT"] for c in range(N_CORES)], axis=1)
    return np.ascontiguousarray(xT.T)[None].astype(np.float32)
```
